# Optimizing a Trainium2 kernel written in Bass

```python
import math
import jax, jax.numpy as jnp
from jax import lax
import numpy as np

D_MODEL = 1024
BATCH = 2
SEQ = 8192
DEPTH = 1
DEC_BATCH = 32
DEC_SEQ = 8
PAST_LEN = 8192
PAGE_SIZE = 128

N_META = 16
ATTN_HEADS = 4
HEAD_DIM = 64
V_DIM = 2 * HEAD_DIM
ATTN_WIDTH = ATTN_HEADS * V_DIM
CONV_WIDTH = D_MODEL - ATTN_WIDTH
QK_COLS = ATTN_HEADS * 2 * HEAD_DIM
IN_COLS = 2 * QK_COLS + ATTN_WIDTH + 3 * CONV_WIDTH
CONV_K = 3
D_FF = 2816
Q_BLOCK = 128
RMS_EPS = 1e-6
SUBLN_EPS = 1e-5
NEG_INF = -1e30

kernel_name = "hymba_diffattn_shortconv_macaron_step"


def rmsnorm(x, g, eps=RMS_EPS):
    xf = x.astype(jnp.float32)
    y = xf * lax.rsqrt(jnp.mean(xf * xf, axis=-1, keepdims=True) + eps) * g.astype(jnp.float32)
    return y.astype(x.dtype)


def swiglu(x, w_gate, w_up, w_down):
    return (jax.nn.silu(x @ w_gate) * (x @ w_up)) @ w_down


def diff_attend(q, k, v, q_pos, k_pos, lam):
    s = jnp.einsum('bqhcd,bkhcd->bhcqk', q.astype(jnp.float32), k.astype(jnp.float32)) * (HEAD_DIM ** -0.5)
    mask = k_pos[None, :] <= q_pos[:, None]
    s = jnp.where(mask, s, NEG_INF)
    p = jax.nn.softmax(s, axis=-1)
    a = p[:, :, 0] - lam * p[:, :, 1]
    return jnp.einsum('bhqk,bkhe->bqhe', a, v.astype(jnp.float32))


def attend_prompt(q, k, v, lam):
    b, t = q.shape[0], q.shape[1]
    s_len = t - N_META
    k_pos = jnp.arange(t)
    kf = k.astype(jnp.float32)
    vf = v.astype(jnp.float32)
    meta_pos = jnp.arange(N_META)
    o_meta = diff_attend(q[:, :N_META], kf[:, :N_META], vf[:, :N_META], meta_pos, meta_pos, lam)
    nb = s_len // Q_BLOCK
    q_blocks = jnp.moveaxis(q[:, N_META:].reshape(b, nb, Q_BLOCK, ATTN_HEADS, 2, HEAD_DIM), 1, 0)
    starts = N_META + jnp.arange(nb) * Q_BLOCK

    def one_block(args):
        qb, s0 = args
        return diff_attend(qb, kf, vf, s0 + jnp.arange(Q_BLOCK), k_pos, lam)

    o_real = lax.map(one_block, (q_blocks, starts))
    o_real = jnp.moveaxis(o_real, 0, 1).reshape(b, s_len, ATTN_HEADS, V_DIM)
    return jnp.concatenate([o_meta, o_real], axis=1)


def attend_sample(q, k, v, lam, k_past, v_past):
    past = k_past.shape[1]
    tq = q.shape[1]
    k_all = jnp.concatenate([k_past, k], axis=1)
    v_all = jnp.concatenate([v_past, v], axis=1)
    k_pos = jnp.arange(past + tq)
    q_pos = past + jnp.arange(tq)
    return diff_attend(q, k_all, v_all, q_pos, k_pos, lam)


def mixer(h, w_in, lq1, lk1, lq2, lk2, subln_g, conv_w, w_out, lambda_init, attend, conv_prev):
    b, t, _ = h.shape
    proj = h @ w_in
    q, k, v, gate_b, gate_c, hc = jnp.split(
        proj, [QK_COLS, 2 * QK_COLS, 2 * QK_COLS + ATTN_WIDTH,
               2 * QK_COLS + ATTN_WIDTH + CONV_WIDTH,
               2 * QK_COLS + ATTN_WIDTH + 2 * CONV_WIDTH], axis=-1)
    q = q.reshape(b, t, ATTN_HEADS, 2, HEAD_DIM)
    k = k.reshape(b, t, ATTN_HEADS, 2, HEAD_DIM)
    v = v.reshape(b, t, ATTN_HEADS, V_DIM)
    lam = (jnp.exp(jnp.sum(lq1.astype(jnp.float32) * lk1.astype(jnp.float32)))
           - jnp.exp(jnp.sum(lq2.astype(jnp.float32) * lk2.astype(jnp.float32))) + lambda_init)
    o = attend(q, k, v, lam)
    o = rmsnorm(o, subln_g, SUBLN_EPS) * (1.0 - lambda_init)
    o = o.reshape(b, t, ATTN_WIDTH).astype(h.dtype)
    u = gate_c * hc
    u_ext = jnp.concatenate([conv_prev.astype(u.dtype), u], axis=1)
    y = sum(conv_w[j] * u_ext[:, j:j + t] for j in range(CONV_K))
    c_out = gate_b * y
    out = jnp.concatenate([o, c_out], axis=-1) @ w_out
    return out, k, v, u_ext[:, -(CONV_K - 1):]


def layer(x, attend, conv_prev, lambda_init,
          f1_pre, f1_wg, f1_wu, f1_wd, f1_post,
          m_pre, w_in, lq1, lk1, lq2, lk2, subln_g, conv_w, w_out, m_post,
          f2_pre, f2_wg, f2_wu, f2_wd, f2_post):
    x = x + 0.5 * rmsnorm(swiglu(rmsnorm(x, f1_pre), f1_wg, f1_wu, f1_wd), f1_post)
    m, k, v, conv_new = mixer(rmsnorm(x, m_pre), w_in, lq1, lk1, lq2, lk2, subln_g, conv_w, w_out,
                              lambda_init, attend, conv_prev)
    x = x + rmsnorm(m, m_post)
    x = x + 0.5 * rmsnorm(swiglu(rmsnorm(x, f2_pre), f2_wg, f2_wu, f2_wd), f2_post)
    return x, k, v, conv_new


def setup_inputs(seed: int = 0) -> dict:
    key = jax.random.key(seed)
    ks = jax.random.split(key, 32)
    n_pages = PAST_LEN // PAGE_SIZE
    n_used = DEC_BATCH * n_pages
    n_pool = n_used + n_used // 4
    f32 = jnp.float32

    def nrm(k, shape, scale):
        return jax.random.normal(k, shape, f32) * scale

    def gain(k):
        return 1.0 + nrm(k, (DEPTH, D_MODEL), 0.05)

    page_table = jax.random.permutation(ks[0], n_pool)[:n_used].reshape(DEC_BATCH, n_pages).astype(jnp.int32)
    return {
        "x_prompt": nrm(ks[1], (BATCH, SEQ, D_MODEL), 1.0),
        "x_sample": nrm(ks[2], (DEC_BATCH, DEC_SEQ, D_MODEL), 1.0),
        "cache_k": nrm(ks[3], (DEPTH, n_pool, PAGE_SIZE, ATTN_HEADS, 2, HEAD_DIM), 1.0),
        "cache_v": nrm(ks[4], (DEPTH, n_pool, PAGE_SIZE, ATTN_HEADS, V_DIM), 1.0),
        "state_conv": nrm(ks[5], (DEPTH, DEC_BATCH, CONV_K - 1, CONV_WIDTH), 1.0),
        "page_table": page_table,
        "meta_tokens": nrm(ks[6], (N_META, D_MODEL), 1.0),
        "ffn1_pre_g": gain(ks[7]),
        "ffn1_w_gate": nrm(ks[8], (DEPTH, D_MODEL, D_FF), D_MODEL ** -0.5),
        "ffn1_w_up": nrm(ks[9], (DEPTH, D_MODEL, D_FF), D_MODEL ** -0.5),
        "ffn1_w_down": nrm(ks[10], (DEPTH, D_FF, D_MODEL), D_FF ** -0.5),
        "ffn1_post_g": gain(ks[11]),
        "mix_pre_g": gain(ks[12]),
        "w_in": nrm(ks[13], (DEPTH, D_MODEL, IN_COLS), D_MODEL ** -0.5),
        "lambda_q1": nrm(ks[14], (DEPTH, HEAD_DIM), 0.1),
        "lambda_k1": nrm(ks[15], (DEPTH, HEAD_DIM), 0.1),
        "lambda_q2": nrm(ks[16], (DEPTH, HEAD_DIM), 0.1),
        "lambda_k2": nrm(ks[17], (DEPTH, HEAD_DIM), 0.1),
        "subln_g": 1.0 + nrm(ks[18], (DEPTH, V_DIM), 0.05),
        "conv_w": nrm(ks[19], (DEPTH, CONV_K, CONV_WIDTH), CONV_K ** -0.5),
        "w_out": nrm(ks[20], (DEPTH, D_MODEL, D_MODEL), D_MODEL ** -0.5),
        "mix_post_g": gain(ks[21]),
        "ffn2_pre_g": gain(ks[22]),
        "ffn2_w_gate": nrm(ks[23], (DEPTH, D_MODEL, D_FF), D_MODEL ** -0.5),
        "ffn2_w_up": nrm(ks[24], (DEPTH, D_MODEL, D_FF), D_MODEL ** -0.5),
        "ffn2_w_down": nrm(ks[25], (DEPTH, D_FF, D_MODEL), D_FF ** -0.5),
        "ffn2_post_g": gain(ks[26]),
    }


def reference(x_prompt, x_sample, cache_k, cache_v, state_conv, page_table, meta_tokens,
              ffn1_pre_g, ffn1_w_gate, ffn1_w_up, ffn1_w_down, ffn1_post_g,
              mix_pre_g, w_in, lambda_q1, lambda_k1, lambda_q2, lambda_k2, subln_g, conv_w, w_out, mix_post_g,
              ffn2_pre_g, ffn2_w_gate, ffn2_w_up, ffn2_w_down, ffn2_post_g):
    b = x_prompt.shape[0]
    db, n_pages = page_table.shape
    past = n_pages * PAGE_SIZE
    meta = jnp.broadcast_to(meta_tokens.astype(x_prompt.dtype)[None], (b, N_META, D_MODEL))
    xp = jnp.concatenate([meta, x_prompt], axis=1)
    xs = x_sample
    kp_l, vp_l, cp_l, ks_l, vs_l, cs_l = [], [], [], [], [], []
    for l in range(DEPTH):
        lambda_init = 0.8 - 0.6 * math.exp(-0.3 * l)
        params = (ffn1_pre_g[l], ffn1_w_gate[l], ffn1_w_up[l], ffn1_w_down[l], ffn1_post_g[l],
                  mix_pre_g[l], w_in[l], lambda_q1[l], lambda_k1[l], lambda_q2[l], lambda_k2[l],
                  subln_g[l], conv_w[l], w_out[l], mix_post_g[l],
                  ffn2_pre_g[l], ffn2_w_gate[l], ffn2_w_up[l], ffn2_w_down[l], ffn2_post_g[l])
        conv0 = jnp.zeros((b, CONV_K - 1, CONV_WIDTH), xp.dtype)
        xp, kp, vp, cp = layer(xp, attend_prompt, conv0, lambda_init, *params)
        k_past = cache_k[l][page_table].reshape(db, past, ATTN_HEADS, 2, HEAD_DIM)
        v_past = cache_v[l][page_table].reshape(db, past, ATTN_HEADS, V_DIM)
        att_s = lambda q, k, v, lam, kp_=k_past, vp_=v_past: attend_sample(q, k, v, lam, kp_, vp_)
        xs, ksn, vsn, csn = layer(xs, att_s, state_conv[l], lambda_init, *params)
        kp_l.append(kp); vp_l.append(vp); cp_l.append(cp)
        ks_l.append(ksn); vs_l.append(vsn); cs_l.append(csn)
    y_prompt = xp[:, N_META:]
    y_sample = xs
    k_prompt_new = jnp.stack(kp_l)
    v_prompt_new = jnp.stack(vp_l)
    conv_prompt_new = jnp.stack(cp_l)
    k_sample_new = jnp.stack(ks_l)
    v_sample_new = jnp.stack(vs_l)
    conv_sample_new = jnp.stack(cs_l)
    return (y_prompt, y_sample, k_prompt_new, v_prompt_new, conv_prompt_new, k_sample_new, v_sample_new, conv_sample_new)
```

```python
import numpy as np
import ml_dtypes
from contextlib import ExitStack
import concourse.bass as bass
import concourse.mybir as mybir
from concourse.bass_utils import run_bass_kernel_spmd

F32 = mybir.dt.float32
BF16 = mybir.dt.bfloat16
I32 = mybir.dt.int32
AF = mybir.ActivationFunctionType
ALU = mybir.AluOpType
IndOff = bass.IndirectOffsetOnAxis if hasattr(bass, "IndirectOffsetOnAxis") else None

D = 1024
DFF = 2816
NF = 22
NH = 4
LAMBDA_INIT = 0.8 - 0.6 * 1.0
RMS_EPS = 1e-6
SUBLN_EPS = 1e-5
N_META = 16


class Cfg:
    def __init__(self, SEQ=8192, PAST=8192, DEC_BATCH=32, BATCH=2, do_sample=True):
        self.SEQ, self.PAST, self.DEC_BATCH, self.BATCH = SEQ, PAST, DEC_BATCH, BATCH
        self.L = SEQ + N_META
        assert SEQ % 2048 == 0
        self.NR = SEQ // 2048
        self.NPG = PAST // 128
        self.SB = DEC_BATCH // 8
        self.NPOOL = (DEC_BATCH * self.NPG) + (DEC_BATCH * self.NPG) // 4
        self.LP = self.NR * 2048 + 128
        self.MP = self.NR * 2048
        self.do_sample = do_sample


class Buf:
    __slots__ = ("name", "w", "r", "dsem", "dcnt", "excl")

    def __init__(self, name, excl=False):
        self.name = name
        self.excl = excl
        self.w = None
        self.r = {}
        self.dsem = None
        self.dcnt = 0


class Eng:
    def __init__(self, h, sem, is_pe=False):
        self.h = h
        self.sem = sem
        self.cnt = 0
        self.waited = {}
        self.is_pe = is_pe


class K:
    def __init__(self, nc, es):
        self.nc = nc
        self.es = es
        self.nsem = 0
        self.pe = Eng(nc.tensor, self.newsem("pe"), True)
        self.act = Eng(nc.scalar, self.newsem("act"))
        self.dve = Eng(nc.vector, self.newsem("dve"))
        self.pool = Eng(nc.gpsimd, self.newsem("pool"))
        self.sp = Eng(nc.sync, self.newsem("sp"))
        self.out_tokens = []
        self.nops = 0
        import os as _os
        self.max_ops = int(_os.environ.get("KMAXOPS", "1000000000"))
        self.dbg = bool(_os.environ.get("KDBG"))

    def newsem(self, name):
        self.nsem += 1
        return self.es.enter_context(self.nc.semaphore(f"s_{name}_{self.nsem}"))

    def _deps(self, eng, reads, writes):
        deps = {}

        def add(tok):
            if tok is None:
                return
            s, v = tok
            k = id(s)
            if k not in deps or deps[k][1] < v:
                deps[k] = (s, v)

        for b in reads:
            add(b.w)
            if b.excl:
                for t in b.r.values():
                    add(t)
        for b in writes:
            add(b.w)
            for t in b.r.values():
                add(t)
        for k, (s, v) in deps.items():
            if eng.is_pe and s is eng.sem:
                continue
            if eng.waited.get(k, 0) >= v:
                continue
            eng.h.wait_ge(s, v)
            eng.waited[k] = v

    def _commit(self, tok, reads, writes):
        for b in writes:
            b.w = tok
            b.r = {}
        for b in reads:
            b.r[id(tok[0])] = tok

    def op(self, eng, fn, reads=(), writes=()):
        self.nops += 1
        if self.nops > self.max_ops:
            return None
        if self.dbg:
            print("OP", self.nops, fn.__code__.co_firstlineno, [b.name for b in reads], [b.name for b in writes])
        self._deps(eng, reads, writes)
        inst = fn(eng.h)
        eng.cnt += 1
        inst.then_inc(eng.sem, 1)
        tok = (eng.sem, eng.cnt)
        self._commit(tok, reads, writes)
        return tok

    def dma(self, eng, fn, sb, reads=(), writes=(), is_out=False):
        self.nops += 1
        if self.nops > self.max_ops:
            return None
        if self.dbg:
            print("DMA", self.nops, fn.__code__.co_firstlineno, [b.name for b in reads], [b.name for b in writes])
        self._deps(eng, reads, writes)
        if sb.dsem is None:
            sb.dsem = self.newsem("d_" + sb.name)
        inst = fn(eng.h)
        sb.dcnt += 16
        inst.then_inc(sb.dsem, 16)
        tok = (sb.dsem, sb.dcnt)
        self._commit(tok, reads, writes)
        if is_out:
            self.out_tokens.append(tok)
        return tok

    def finish(self):
        last = {}
        for s, v in self.out_tokens:
            if id(s) not in last or last[id(s)][1] < v:
                last[id(s)] = (s, v)
        for s, v in last.values():
            self.sp.h.wait_ge(s, v)


def build(cfg):
    nc = bass.Bass("TRN2", target_bir_lowering=False)
    es = ExitStack()
    k = K(nc, es)
    NR, NPG, SB, LP = cfg.NR, cfg.NPG, cfg.SB, cfg.LP

    def din(name, shape, dt=F32):
        return nc.dram_tensor(name, list(shape), dt, kind="ExternalInput")

    def dout(name, shape, dt=F32):
        return nc.dram_tensor(name, list(shape), dt, kind="ExternalOutput")

    x_pre = din("x_pre", [NR * 3 * 512, D])
    x_own = din("x_own", [NR * 512, D])
    x_halo = din("x_halo", [128, D])
    x_smp = din("x_smp", [128, D])
    dmask = din("dmask", [128, 4 * 512], BF16)
    sel_in = din("sel", [128, 4])
    smask_in = din("smask", [128, 4 * 64], BF16)
    W = {}
    for nm, shp in [("f1g", [D, DFF]), ("f1u", [D, DFF]), ("f1d", [DFF, D]), ("win", [D, 3072]),
                    ("wout", [D, D]), ("f2g", [D, DFF]), ("f2u", [D, DFF]), ("f2d", [DFF, D])]:
        W[nm] = din(nm, shp)
    V = {}
    for nm in ["f1pre", "f1post", "mpre", "mpost", "f2pre", "f2post"]:
        V[nm] = din(nm, [1, D])
    subln_in = din("subln", [1, 128])
    convw_in = din("convw", [3, 512])
    lam_in = din("lamv", [1, 256])
    if cfg.do_sample:
        cache_k = din("cache_k", [cfg.NPOOL * 128, 512])
        cache_v = din("cache_v", [cfg.NPOOL * 128, 512])
        pt_in = din("pt", [1, SB * NPG], I32)
        sconv_in = din("sconv", [SB * 2, 512])

    y_own = dout("y_own", [NR * 512, D])
    k_own = dout("k_own", [NR * 512, 512])
    v_own = dout("v_own", [NR * 512, 512])
    conv_last = dout("conv_last", [2, 512])
    y_smp = dout("y_smp", [128, D])
    k_smp = dout("k_smp", [128, 512])
    v_smp = dout("v_smp", [128, 512])
    conv_smp = dout("conv_smp", [SB * 2, 512])

    KTs = nc.dram_tensor("KTs", [512, LP], BF16)
    Vs = nc.dram_tensor("Vs", [LP, 512], BF16)
    KTs_b = {}
    Vs_b = {}

    def sb(name, shape, dt):
        return nc.alloc_sbuf_tensor("sb_" + name, list(shape), dt)

    xt = [sb("xt0", [128, 4, D], F32)] * 2
    xt_b = [[Buf(f"xt0_{j}") for j in range(4)]] * 2
    junk = sb("junk", [128, D], BF16)
    junk_b = Buf("junk")
    xs = [sb(f"xs{i}", [128, D], BF16) for i in range(2)]
    xs_b = [Buf(f"xs{i}") for i in range(2)]
    xnT = sb("xnT", [128, 8, 512], BF16)
    xnT_b = [Buf(f"xnT{i}") for i in range(8)]
    act = sb("act", [128, NF, 512], BF16)
    act_b = [Buf(f"act{i}") for i in range(NF)]
    sg = [sb(f"sg{i}", [128, 512], BF16) for i in range(2)]
    sg_b = [Buf(f"sg{i}") for i in range(2)]
    NSLAB = 4
    slab2 = [sb(f"slabp{i}", [128, 2, 8, 512], BF16) for i in range(NSLAB // 2)]
    slab = [slab2[i // 2][:, i % 2] for i in range(NSLAB)]
    slab_b = [Buf(f"slab{i}") for i in range(NSLAB)]
    wd = [sb(f"wd{i}", [128, 5, D], BF16) for i in range(2)]
    wd_b = [Buf(f"wd{i}") for i in range(2)]
    ytmp = [sb(f"ytmp{i}", [128, 512], F32) for i in range(2)]
    ytmp_b = [Buf(f"ytmp{i}") for i in range(2)]
    gpost = {nm: sb("gb_" + nm, [128, D], F32) for nm in ["f1post", "mpost", "f2post"]}
    gcol = {nm: sb("gc_" + nm, [128, 8], F32) for nm in ["f1pre", "mpre", "f2pre"]}
    const_b = Buf("consts")
    stat = sb("stat", [128, 64], F32)
    stat_b = [Buf(f"stat{i}") for i in range(16)]
    KT_sb = sb("KT_sb", [128, 4, 512], BF16)
    KT_sb_b = Buf("KT_sb")
    V_sb = sb("V_sb", [128, 4, 512], BF16)
    V_sb_b = Buf("V_sb")
    kf = [sb(f"kf{i}", [128, 512], F32) for i in range(2)]
    kf_b = [Buf(f"kf{i}") for i in range(2)]
    qT = sb("qT", [128, 4, 512], BF16)
    qT_b = [Buf(f"qT{i}") for i in range(4)]
    uT = sb("uT", [128, 4, 516], F32)
    uT_b = [Buf(f"uT{i}") for i in range(4)]
    gcs = [sb(f"gcs{i}", [128, 512], F32) for i in range(2)]
    gcs_b = [Buf(f"gcs{i}") for i in range(2)]
    cvt = [sb(f"cvt{i}", [128, 512], F32) for i in range(2)]
    cvt_b = [Buf(f"cvt{i}") for i in range(2)]
    catT = sb("catT", [128, 8, 512], BF16)
    catT_b = [Buf(f"catT{i}") for i in range(8)]
    halo_u = sb("halo_u", [128, 4, 16 if cfg.do_sample else 128], F32)
    halo_b = Buf("halo_u")
    convw = sb("convw", [128, 4, 3], F32)
    subln_g = sb("subln_g", [128, 128], F32)
    lamw = sb("lamw", [128, 256], F32)
    lamt = sb("lamt", [128, 8], F32)
    sel = sb("sel", [128, 4], F32)
    mask0 = sb("mask0", [128, 4, 512], BF16)
    ident = sb("ident", [128, 128], BF16)
    epst = sb("epst", [128, 2], F32)
    ones_bf = sb("ones_bf", [128, 128], BF16)
    ones4 = sb("ones4", [128, 4, 1], BF16)
    ident_f = sb("ident_f", [128, 128], F32)
    NKC = 3
    KTc = [sb(f"KTc{i}", [128, 512], BF16) for i in range(NKC)]
    KTc_b = [Buf(f"KTc{i}") for i in range(NKC)]
    Vc = [sb(f"Vc{i}", [128, 4, 132], BF16) for i in range(NKC)]
    Vc_b = [Buf(f"Vc{i}") for i in range(NKC)]
    PT = [sb(f"PT{i}", [128, 512], BF16) for i in range(4)]
    PT_b = [Buf(f"PT{i}") for i in range(4)]
    osb = [sb(f"osb{i}", [128, 128], F32) for i in range(2)]
    osb_b = [Buf(f"osb{i}") for i in range(2)]
    onb = [sb(f"onb{i}", [128, 128], BF16) for i in range(2)]
    onb_b = [Buf(f"onb{i}") for i in range(2)]

    if cfg.do_sample:
        NPP = SB * NPG // 2
        pti = sb("pti", [128, NPP], I32)
        idx = sb("idx", [128, NPP], I32)
        iota_f = sb("iota_f", [128, 1], F32)
        smask = sb("smask", [128, 4, 64], BF16)
        kpg = [sb(f"kpg{i}", [128, 2, 512], F32) for i in range(2)]
        kpg_b = [Buf(f"kpg{i}") for i in range(2)]
        vpg = [sb(f"vpg{i}", [128, 2, 512], F32) for i in range(2)]
        vpg_b = [Buf(f"vpg{i}") for i in range(2)]
        KTp = [sb(f"KTp{i}", [128, 4, 128], BF16) for i in range(2)]
        KTp_b = [Buf(f"KTp{i}") for i in range(2)]
        Vp = [sb(f"Vp{i}", [128, 4, 132], BF16) for i in range(2)]
        Vp_b = [Buf(f"Vp{i}") for i in range(2)]
        Vn = sb("Vn", [128, 4, 132], BF16)
        Vn_b = Buf("Vn")
        PTs = [sb(f"PTs{i}", [128, 64], BF16) for i in range(2)]
        PTs_b = [Buf(f"PTs{i}") for i in range(2)]
        Qblk = sb("Qblk", [128, 4, 4, 16], BF16)
        Qblk_b = Buf("Qblk")
    print("sbuf bytes remaining", nc.sbuf_bytes_remaining)
    PS = [nc.alloc_psum_tensor(f"ps{i}", [128, 512], F32) for i in range(8)]
    PS_b = [Buf(f"ps{i}", excl=True) for i in range(8)]

    pe, actE, dve, pool, sp = k.pe, k.act, k.dve, k.pool, k.sp
    nc_allow = nc.allow_non_contiguous_dma(reason="tiny constant / transposed vector loads")
    es.enter_context(nc_allow)
    es.enter_context(nc.allow_low_precision(reason="bf16 matmul operands, fp32 accumulate"))

    for nm in gpost:
        k.dma(sp, lambda e, nm=nm: e.dma_start(out=gpost[nm][:], in_=V[nm][0:1, :].broadcast_to([128, D])),
              const_b, writes=[const_b])
    for nm in gcol:
        k.dma(sp, lambda e, nm=nm: e.dma_start(out=gcol[nm][:], in_=V[nm].ap().rearrange("o (k p) -> p (o k)", p=128)),
              const_b, writes=[const_b])
    for j3 in range(3):
        k.dma(sp, lambda e, j3=j3: e.dma_start(out=convw[:, :, j3], in_=convw_in[j3:j3 + 1, :].rearrange("o (c p) -> p (o c)", p=128)),
              const_b, writes=[const_b])
    k.dma(sp, lambda e: e.dma_start(out=subln_g[:], in_=subln_in[0:1, :].broadcast_to([128, 128])),
          const_b, writes=[const_b])
    k.dma(sp, lambda e: e.dma_start(out=lamw[:], in_=lam_in[0:1, :].broadcast_to([128, 256])),
          const_b, writes=[const_b])
    k.dma(sp, lambda e: e.dma_start(out=sel[:], in_=sel_in.ap()), const_b, writes=[const_b])
    k.dma(sp, lambda e: e.dma_start(out=mask0[:], in_=dmask.ap().rearrange("p (a q) -> p a q", a=4)),
          const_b, writes=[const_b])
    if cfg.do_sample:
        k.dma(sp, lambda e: e.dma_start(out=smask[:], in_=smask_in.ap().rearrange("p (a q) -> p a q", a=4)),
              const_b, writes=[const_b])
        ptv = pt_in.ap().rearrange("o (n two) -> o n two", two=2)
        k.dma(sp, lambda e: e.dma_start(out=pti[0:64, :], in_=ptv[0:1, :, 0].broadcast_to([64, NPP])), const_b, writes=[const_b])
        k.dma(sp, lambda e: e.dma_start(out=pti[64:128, :], in_=ptv[0:1, :, 1].broadcast_to([64, NPP])), const_b, writes=[const_b])
        k.op(pool, lambda e: e.iota(out=iota_f[0:64, :], pattern=[[0, 1]], base=0, channel_multiplier=1,
                                    allow_small_or_imprecise_dtypes=True), writes=[const_b])
        k.op(pool, lambda e: e.iota(out=iota_f[64:128, :], pattern=[[0, 1]], base=0, channel_multiplier=1,
                                    allow_small_or_imprecise_dtypes=True), writes=[const_b])
        k.op(dve, lambda e: e.tensor_scalar(out=idx[:], in0=pti[:], scalar1=64.0, scalar2=iota_f[:, 0:1],
                                            op0=ALU.mult, op1=ALU.add), reads=[const_b], writes=[const_b])
        for i in range(2):
            k.op(pool, lambda e, i=i: e.memset(Vp[i][:], 1.0), writes=[Vp_b[i]])
        k.op(pool, lambda e: e.memset(Vn[:], 1.0), writes=[Vn_b])
    k.op(pool, lambda e: e.memset(epst[:, 0:1], RMS_EPS), writes=[const_b])
    k.op(pool, lambda e: e.memset(epst[:, 1:2], SUBLN_EPS), writes=[const_b])
    for nm in ("f1post", "f2post"):
        k.op(dve, lambda e, nm=nm: e.tensor_scalar(out=gpost[nm][:], in0=gpost[nm][:], scalar1=0.5, scalar2=None, op0=ALU.mult),
             reads=[const_b], writes=[const_b])
    k.op(pool, lambda e: e.memset(ones_bf[:], 1.0), writes=[const_b])
    k.op(pool, lambda e: e.memset(ones4[:], 1.0), writes=[const_b])
    k.op(pool, lambda e: e.affine_select(out=ident[:], in_=ones_bf[:], pattern=[[1, 128]],
                                         compare_op=ALU.is_equal, fill=0.0, base=0, channel_multiplier=-1),
         reads=[const_b], writes=[const_b])
    k.op(dve, lambda e: e.tensor_copy(out=ident_f[:], in_=ident[:]), reads=[const_b], writes=[const_b])
    for i in range(NKC):
        k.op(pool, lambda e, i=i: e.memset(Vc[i][:], 1.0), writes=[Vc_b[i]])
    k.op(dve, lambda e: e.scalar_tensor_tensor(out=lamw[:, 0:64], in0=lamw[:, 0:64], scalar=1.0, in1=lamw[:, 64:128],
                                               op0=ALU.mult, op1=ALU.mult, accum_out=lamt[:, 0:1]),
         reads=[const_b], writes=[const_b])
    k.op(dve, lambda e: e.scalar_tensor_tensor(out=lamw[:, 128:192], in0=lamw[:, 128:192], scalar=1.0, in1=lamw[:, 192:256],
                                               op0=ALU.mult, op1=ALU.mult, accum_out=lamt[:, 1:2]),
         reads=[const_b], writes=[const_b])
    k.op(actE, lambda e: e.activation(out=lamt[:, 2:4], in_=lamt[:, 0:2], func=AF.Exp), reads=[const_b], writes=[const_b])
    k.op(dve, lambda e: e.tensor_tensor(out=lamt[:, 4:5], in0=lamt[:, 3:4], in1=lamt[:, 2:3], op=ALU.subtract),
         reads=[const_b], writes=[const_b])
    k.op(dve, lambda e: e.tensor_scalar(out=lamt[:, 4:5], in0=lamt[:, 4:5], scalar1=-LAMBDA_INIT, scalar2=None, op0=ALU.add),
         reads=[const_b], writes=[const_b])
    k.op(dve, lambda e: e.tensor_scalar(out=subln_g[:], in0=subln_g[:], scalar1=1.0 - LAMBDA_INIT, scalar2=None, op0=ALU.mult),
         reads=[const_b], writes=[const_b])

    state = {"slab": 0, "wd": 0, "ps": 0, "stat": 0, "sg": 0, "ytmp": 0, "xs": 0, "kf": 0, "gcs": 0, "cvt": 0,
             "kc": 0, "pt": 0, "osb": 0, "pts": 0, "kpg": 0, "ktp": 0}

    ring = {"banks": list(range(8))}

    def rr(key, n):
        v = state[key]
        if key == "ps":
            b = ring["banks"]
            state[key] = (v + 1) % len(b)
            return b[v % len(b)]
        state[key] = (v + 1) % n
        return v

    wscr = {}

    def load_slab(wname, col0, ncols):
        i = rr("slab", NSLAB)
        key = (wname, col0)
        if key not in wscr:
            src = W[wname][:, col0:col0 + ncols].rearrange("(k p) c -> p k c", p=128)
            k.dma(pool, lambda e: e.dma_start(out=slab[i][:, :, 0:ncols], in_=src), slab_b[i], writes=[slab_b[i]])
            scr = nc.dram_tensor(f"ws_{wname}_{col0}", [128, 8 * ncols], BF16)
            sbuf_ = Buf(f"ws_{wname}_{col0}")
            wscr[key] = (scr, sbuf_)
            k.dma(sp, lambda e: e.dma_start(out=scr.ap().rearrange("p (k c) -> p k c", k=8), in_=slab[i][:, :, 0:ncols]),
                  sbuf_, reads=[slab_b[i]], writes=[sbuf_])
        else:
            scr, sbuf_ = wscr[key]
            k.dma(pool, lambda e: e.dma_start(out=slab[i][:, :, 0:ncols], in_=scr.ap().rearrange("p (k c) -> p k c", k=8)),
                  slab_b[i], reads=[sbuf_], writes=[slab_b[i]])
        return i

    def load_slab_pair(wa, wb, col0, ncols):
        if state["slab"] % 2 == 1:
            rr("slab", NSLAB)
        ia = rr("slab", NSLAB)
        ib = rr("slab", NSLAB)
        pi = ia // 2
        key = (wa, wb, col0)
        if key not in wscr:
            for (wn, ii) in ((wa, ia), (wb, ib)):
                src = W[wn][:, col0:col0 + ncols].rearrange("(k p) c -> p k c", p=128)
                k.dma(pool, lambda e, ii=ii, src=src: e.dma_start(out=slab[ii][:, :, 0:ncols], in_=src), slab_b[ii], writes=[slab_b[ii]])
            scr = nc.dram_tensor(f"wsp_{wa}_{col0}", [128, 2 * 8 * ncols], BF16)
            sbuf_ = Buf(f"wsp_{wa}_{col0}")
            wscr[key] = (scr, sbuf_)
            k.dma(sp, lambda e: e.dma_start(out=scr.ap().rearrange("p (t k c) -> p t k c", t=2, k=8), in_=slab2[pi][:, :, :, 0:ncols]),
                  sbuf_, reads=[slab_b[ia], slab_b[ib]], writes=[sbuf_])
        else:
            scr, sbuf_ = wscr[key]
            k.dma(pool, lambda e: e.dma_start(out=slab2[pi][:, :, :, 0:ncols], in_=scr.ap().rearrange("p (t k c) -> p t k c", t=2, k=8)),
                  slab_b[ia], reads=[sbuf_], writes=[slab_b[ia], slab_b[ib]])
        return ia, ib

    def load_wd(wname, f0, f1):
        i = rr("wd", 2)
        nf = f1 - f0
        key = (wname, f0)
        if key not in wscr:
            src = W[wname][f0 * 128:f1 * 128, :].rearrange("(f p) c -> p f c", p=128)
            k.dma(pool, lambda e: e.dma_start(out=wd[i][:, 0:nf, :], in_=src), wd_b[i], writes=[wd_b[i]])
            scr = nc.dram_tensor(f"ws_{wname}_{f0}", [128, nf * D], BF16)
            sbuf_ = Buf(f"ws_{wname}_{f0}")
            wscr[key] = (scr, sbuf_)
            k.dma(sp, lambda e: e.dma_start(out=scr.ap().rearrange("p (f c) -> p f c", f=nf), in_=wd[i][:, 0:nf, :]),
                  sbuf_, reads=[wd_b[i]], writes=[sbuf_])
        else:
            scr, sbuf_ = wscr[key]
            k.dma(pool, lambda e: e.dma_start(out=wd[i][:, 0:nf, :], in_=scr.ap().rearrange("p (f c) -> p f c", f=nf)),
                  wd_b[i], reads=[sbuf_], writes=[wd_b[i]])
        return i

    def new_stat():
        i = rr("stat", 16)
        return stat[:, 4 * i:4 * i + 4], stat_b[i]

    def rstd_from_ssq(ssq_ap, ssq_buf, n, eps, mul, ncols, rows=128):
        ecol = 0 if eps == RMS_EPS else 1
        k.op(actE, lambda e: e.activation(out=ssq_ap, in_=ssq_ap, func=AF.Sqrt, bias=epst[0:rows, ecol:ecol + 1],
                                          scale=1.0 / n), reads=[ssq_buf, const_b], writes=[ssq_buf])
        k.op(dve, lambda e: e.reciprocal(out=ssq_ap, in_=ssq_ap), reads=[ssq_buf], writes=[ssq_buf])

    def norm_transpose(xi, nb, gname):
        T = nb * 128
        st, st_b = new_stat()
        for blk in range(nb):
            k.op(actE, lambda e, blk=blk: e.activation(out=junk[:], in_=xt[xi][:, blk, :], func=AF.Square,
                                                       accum_out=st[:, blk:blk + 1]),
                 reads=[xt_b[xi][blk]], writes=[junk_b, st_b])
        rstd_from_ssq(st[:, 0:nb], st_b, D, RMS_EPS, 1.0, nb)
        for blk in range(nb):
            j = blk % 2
            k.op(actE, lambda e, blk=blk, j=j: e.activation(out=xs[j][:], in_=xt[xi][:, blk, :], func=AF.Copy,
                                                            scale=st[:, blk:blk + 1]),
                 reads=[xt_b[xi][blk], st_b], writes=[xs_b[j]])
            for half in range(2):
                p = rr("ps", 8)
                pbf = PS[p][:].bitcast(BF16)

                def tr(e, j=j, half=half, pbf=pbf):
                    last = None
                    for q in range(4):
                        kk = half * 4 + q
                        last = e.transpose(out=pbf[:, q * 128:(q + 1) * 128], in_=xs[j][:, kk * 128:(kk + 1) * 128],
                                           identity=ident[:])
                    return last
                k.op(pe, tr, reads=[xs_b[j], const_b], writes=[PS_b[p]])
                for q in range(4):
                    kk = half * 4 + q
                    eng = dve if (q % 2 == 0) else actE
                    if eng is dve:
                        k.op(dve, lambda e, kk=kk, q=q, blk=blk, pbf=pbf: e.tensor_scalar(
                            out=xnT[:, kk, blk * 128:(blk + 1) * 128], in0=pbf[:, q * 128:(q + 1) * 128],
                            scalar1=gcol[gname][:, kk:kk + 1], scalar2=None, op0=ALU.mult),
                             reads=[PS_b[p], const_b], writes=[xnT_b[kk]])
                    else:
                        k.op(actE, lambda e, kk=kk, q=q, blk=blk, pbf=pbf: e.activation(
                            out=xnT[:, kk, blk * 128:(blk + 1) * 128], in_=pbf[:, q * 128:(q + 1) * 128],
                            func=AF.Copy, scale=gcol[gname][:, kk:kk + 1]),
                             reads=[PS_b[p], const_b], writes=[xnT_b[kk]])

    def post_norm_residual(xi, blk, pA, pB, gname, mul):
        st, st_b = new_stat()
        for h2, p in enumerate((pA, pB)):
            k.op(actE, lambda e, h2=h2, p=p: e.activation(out=junk[:, 0:512], in_=PS[p][:], func=AF.Square,
                                                          accum_out=st[:, h2:h2 + 1]),
                 reads=[PS_b[p]], writes=[junk_b, st_b])
        k.op(dve, lambda e: e.tensor_tensor(out=st[:, 2:3], in0=st[:, 0:1], in1=st[:, 1:2], op=ALU.add),
             reads=[st_b], writes=[st_b])
        rstd_from_ssq(st[:, 2:3], st_b, D, RMS_EPS, mul, 1)
        for h2, p in enumerate((pA, pB)):
            t = rr("ytmp", 2)
            k.op(dve, lambda e, h2=h2, p=p, t=t: e.tensor_tensor(out=ytmp[t][:], in0=PS[p][:],
                                                                 in1=gpost[gname][:, h2 * 512:(h2 + 1) * 512], op=ALU.mult),
                 reads=[PS_b[p], const_b], writes=[ytmp_b[t]])
            k.op(dve, lambda e, h2=h2, t=t: e.scalar_tensor_tensor(
                out=xt[xi][:, blk, h2 * 512:(h2 + 1) * 512], in0=ytmp[t][:], scalar=st[:, 2:3],
                in1=xt[xi][:, blk, h2 * 512:(h2 + 1) * 512], op0=ALU.mult, op1=ALU.add),
                 reads=[ytmp_b[t], st_b, xt_b[xi][blk]], writes=[xt_b[xi][blk]])

    def ffn(xi, nb, wg, wu, wdn, gpre, gpostn):
        T = nb * 128
        norm_transpose(xi, nb, gpre)
        for fg in range(6):
            nf = 4 if fg < 5 else 2
            sgi, sui = load_slab_pair(wg, wu, fg * 512, nf * 128)
            for fi in range(nf):
                f = fg * 4 + fi
                pg = rr("ps", 8)
                pu = rr("ps", 8)

                def mm(e, si, p, fi=fi):
                    last = None
                    for kk in range(8):
                        last = e.matmul(out=PS[p][:, 0:T], lhsT=slab[si][:, kk, fi * 128:(fi + 1) * 128],
                                        rhs=xnT[:, kk, 0:T], start=(kk == 0), stop=(kk == 7))
                    return last
                k.op(pe, lambda e, p=pg: mm(e, sgi, p), reads=[slab_b[sgi]] + xnT_b, writes=[PS_b[pg]])
                k.op(pe, lambda e, p=pu: mm(e, sui, p), reads=[slab_b[sui]] + xnT_b, writes=[PS_b[pu]])
                s = rr("sg", 2)
                k.op(actE, lambda e, s=s, p=pg: e.activation(out=sg[s][:, 0:T], in_=PS[p][:, 0:T], func=AF.Silu),
                     reads=[PS_b[pg]], writes=[sg_b[s]])
                k.op(dve, lambda e, s=s, p=pu, f=f: e.tensor_tensor(out=act[:, f, 0:T], in0=sg[s][:, 0:T],
                                                                    in1=PS[p][:, 0:T], op=ALU.mult),
                     reads=[sg_b[s], PS_b[pu]], writes=[act_b[f]])
        pys = {}
        for blk in range(nb):
            for half in range(2):
                pys[(blk, half)] = rr("ps", 8)
        for (f0, f1) in [(0, 5), (5, 10), (10, 14), (14, 18), (18, 22)]:
            wi = load_wd(wdn, f0, f1)
            for blk in range(nb):
                for half in range(2):
                    p = pys[(blk, half)]

                    def mmd(e, blk=blk, half=half, p=p, wi=wi, f0=f0, f1=f1):
                        last = None
                        for f in range(f0, f1):
                            last = e.matmul(out=PS[p][:], lhsT=act[:, f, blk * 128:(blk + 1) * 128],
                                            rhs=wd[wi][:, f - f0, half * 512:(half + 1) * 512],
                                            start=(f == 0), stop=(f == NF - 1))
                        return last
                    k.op(pe, mmd, reads=[wd_b[wi]] + act_b[f0:f1], writes=[PS_b[p]])
        for blk in range(nb):
            post_norm_residual(xi, blk, pys[(blk, 0)], pys[(blk, 1)], gpostn, 0.5)
        return

    def mm_fm(si, c0, p, T):
        def f(e):
            last = None
            for kk in range(8):
                last = e.matmul(out=PS[p][:, 0:T], lhsT=slab[si][:, kk, c0:c0 + 128], rhs=xnT[:, kk, 0:T],
                                start=(kk == 0), stop=(kk == 7))
            return last
        k.op(pe, f, reads=[slab_b[si]] + xnT_b, writes=[PS_b[p]])

    def mm_tm(si, blk, p):
        def f(e):
            last = None
            for kk in range(8):
                last = e.matmul(out=PS[p][:], lhsT=xnT[:, kk, blk * 128:(blk + 1) * 128], rhs=slab[si][:, kk, :],
                                start=(kk == 0), stop=(kk == 7))
            return last
        k.op(pe, f, reads=[slab_b[si]] + xnT_b, writes=[PS_b[p]])

    def tile_pass(xsrc, nb, mode, xi, kpos=None, outs=None, halo_cols=None, smp=False):
        T = nb * 128
        blks = [xt_b[xi][j] for j in range(nb)]
        k.dma(sp, lambda e: e.dma_start(out=xt[xi][:, 0:nb, :], in_=xsrc.rearrange("(b p) d -> p b d", p=128)),
              blks[0], writes=blks)
        ffn(xi, nb, "f1g", "f1u", "f1d", "f1pre", "f1post")
        norm_transpose(xi, nb, "mpre")
        own = mode == "own"
        if mode in ("pre", "own"):
            sk = load_slab("win", 512, 512)
            for h in range(NH):
                p = rr("ps", 8)
                mm_fm(sk, h * 128, p, T)
                eng = dve if h % 2 == 0 else actE
                if eng is dve:
                    k.op(dve, lambda e, h=h, p=p: e.tensor_copy(out=KT_sb[:, h, 0:T], in_=PS[p][:, 0:T]),
                         reads=[PS_b[p]], writes=[KT_sb_b])
                else:
                    k.op(actE, lambda e, h=h, p=p: e.activation(out=KT_sb[:, h, 0:T], in_=PS[p][:, 0:T], func=AF.Copy),
                         reads=[PS_b[p]], writes=[KT_sb_b])
            if not smp:
                rb = Buf(f"KTs_{kpos}")
                KTs_b[kpos] = rb
                k.dma(sp, lambda e: e.dma_start(out=KTs[:, kpos:kpos + T].rearrange("(h p) t -> p h t", p=128),
                                                in_=KT_sb[:, :, 0:T]), KT_sb_b, reads=[KT_sb_b], writes=[rb])
            else:
                rb = Buf("KTs_meta")
                KTs_b[cfg.MP] = rb
                k.dma(sp, lambda e: e.dma_start(out=KTs[:, cfg.MP:cfg.MP + 16].rearrange("(h p) t -> p h t", p=128),
                                                in_=KT_sb[:, :, 40:56]), KT_sb_b, reads=[KT_sb_b], writes=[rb])
            if own:
                for blk in range(nb):
                    p = rr("ps", 8)
                    mm_tm(sk, blk, p)
                    j = rr("kf", 2)
                    k.op(actE, lambda e, p=p, j=j: e.activation(out=kf[j][:], in_=PS[p][:], func=AF.Copy),
                         reads=[PS_b[p]], writes=[kf_b[j]])
                    k.dma(sp, lambda e, blk=blk, j=j: e.dma_start(out=outs["k"][blk * 128:(blk + 1) * 128, :], in_=kf[j][:]),
                          kf_b[j], reads=[kf_b[j]], is_out=True)
            sv = load_slab("win", 1024, 512)
            for blk in range(nb):
                p = rr("ps", 8)
                mm_tm(sv, blk, p)
                k.op(dve, lambda e, blk=blk, p=p: e.tensor_copy(out=V_sb[:, blk, :], in_=PS[p][:]),
                     reads=[PS_b[p]], writes=[V_sb_b])
                if own:
                    j = rr("kf", 2)
                    k.op(actE, lambda e, p=p, j=j: e.activation(out=kf[j][:], in_=PS[p][:], func=AF.Copy),
                         reads=[PS_b[p]], writes=[kf_b[j]])
                    k.dma(sp, lambda e, blk=blk, j=j: e.dma_start(out=outs["v"][blk * 128:(blk + 1) * 128, :], in_=kf[j][:]),
                          kf_b[j], reads=[kf_b[j]], is_out=True)
            if not smp:
                rb = Buf(f"Vs_{kpos}")
                Vs_b[kpos] = rb
                k.dma(sp, lambda e: e.dma_start(out=Vs[kpos:kpos + T, :].rearrange("(b p) c -> p b c", p=128),
                                                in_=V_sb[:, 0:nb, :]), V_sb_b, reads=[V_sb_b], writes=[rb])
            else:
                rb = Buf("Vs_meta")
                Vs_b[cfg.MP] = rb
                k.dma(sp, lambda e: e.dma_start(out=Vs[cfg.MP:cfg.MP + 16, :], in_=V_sb[40:56, 0, :]),
                      V_sb_b, reads=[V_sb_b], writes=[rb])
        if mode in ("halo", "own"):
            sgc = load_slab("win", 2048, 512)
            shc = load_slab("win", 2560, 512)
            for cc in range(4):
                p1 = rr("ps", 8)
                p2 = rr("ps", 8)
                mm_fm(sgc, cc * 128, p1, T)
                mm_fm(shc, cc * 128, p2, T)
                j = rr("gcs", 2)
                k.op(actE, lambda e, p=p1, j=j: e.activation(out=gcs[j][:, 0:T], in_=PS[p][:, 0:T], func=AF.Copy),
                     reads=[PS_b[p1]], writes=[gcs_b[j]])
                if mode == "halo":
                    k.op(dve, lambda e, p=p2, j=j, cc=cc: e.tensor_tensor(out=halo_u[:, cc, 0:T], in0=gcs[j][:, 0:T],
                                                                          in1=PS[p][:, 0:T], op=ALU.mult),
                         reads=[gcs_b[j], PS_b[p2]], writes=[halo_b])
                elif smp:
                    k.op(dve, lambda e, p=p2, j=j, cc=cc: e.tensor_tensor(
                        out=uT[:, cc, 0:136].rearrange("p (s t) -> p s t", t=34)[:, :, 2:34],
                        in0=gcs[j][:, 0:128].rearrange("p (s t) -> p s t", t=32),
                        in1=PS[p][:, 0:128].rearrange("p (s t) -> p s t", t=32), op=ALU.mult),
                         reads=[gcs_b[j], PS_b[p2]], writes=[uT_b[cc]])
                    k.op(dve, lambda e, cc=cc: e.tensor_copy(out=halo_u[:, cc, 0:2 * NR],
                                                             in_=uT[:, cc, 2 + 8:2 + 8 + 2 * NR]),
                         reads=[uT_b[cc]], writes=[halo_b])
                else:
                    k.op(dve, lambda e, p=p2, j=j, cc=cc: e.tensor_tensor(out=uT[:, cc, 2:2 + T], in0=gcs[j][:, 0:T],
                                                                          in1=PS[p][:, 0:T], op=ALU.mult),
                         reads=[gcs_b[j], PS_b[p2]], writes=[uT_b[cc]])
        if not own:
            return
        sq = load_slab("win", 0, 512)
        for h in range(NH):
            p = rr("ps", 8)
            mm_fm(sq, h * 128, p, T)
            k.op(actE, lambda e, h=h, p=p: e.activation(out=qT[:, h, 0:T], in_=PS[p][:, 0:T], func=AF.Copy, scale=0.125),
                 reads=[PS_b[p]], writes=[qT_b[h]])
        sgb = load_slab("win", 1536, 512)
        for cc in range(4):
            if not smp:
                k.op(dve, lambda e, cc=cc: e.tensor_copy(out=uT[:, cc, 0:2], in_=halo_u[:, cc, halo_cols:halo_cols + 2]),
                     reads=[halo_b], writes=[uT_b[cc]])
            else:
                for s_ in range(SB):
                    k.dma(sp, lambda e, cc=cc, s_=s_: e.dma_start(
                        out=uT[:, cc, s_ * 34:s_ * 34 + 2],
                        in_=sconv_in[2 * s_:2 * s_ + 2, cc * 128:(cc + 1) * 128].rearrange("j p -> p j")),
                          uT_b[cc], writes=[uT_b[cc]])
            a = rr("cvt", 2)
            if not smp:
                def uv(o):
                    return uT[:, cc, o:o + T]
                cva = cvt[a][:, 0:T]
            else:
                def uv(o, cc=cc):
                    return uT[:, cc, 0:4 * 34].rearrange("p (s t) -> p s t", t=34)[:, :, o:o + 32]
                cva = cvt[a][:, 0:128].rearrange("p (s t) -> p s t", t=32)
            k.op(dve, lambda e, cc=cc: e.tensor_scalar(out=cva, in0=uv(0), scalar1=convw[:, cc, 0:1], scalar2=None,
                                                       op0=ALU.mult), reads=[uT_b[cc], const_b], writes=[cvt_b[a]])
            k.op(dve, lambda e, cc=cc: e.scalar_tensor_tensor(out=cva, in0=uv(1), scalar=convw[:, cc, 1:2], in1=cva,
                                                              op0=ALU.mult, op1=ALU.add),
                 reads=[uT_b[cc], const_b, cvt_b[a]], writes=[cvt_b[a]])
            k.op(dve, lambda e, cc=cc: e.scalar_tensor_tensor(out=cva, in0=uv(2), scalar=convw[:, cc, 2:3], in1=cva,
                                                              op0=ALU.mult, op1=ALU.add),
                 reads=[uT_b[cc], const_b, cvt_b[a]], writes=[cvt_b[a]])
            p = rr("ps", 8)
            mm_fm(sgb, cc * 128, p, T)
            k.op(dve, lambda e, cc=cc, p=p, a=a: e.tensor_tensor(out=catT[:, 4 + cc, 0:T], in0=cvt[a][:, 0:T],
                                                                 in1=PS[p][:, 0:T], op=ALU.mult),
                 reads=[cvt_b[a], PS_b[p]], writes=[catT_b[4 + cc]])
        if "conv" in outs and not smp:
            for j2 in range(2):
                k.dma(sp, lambda e, j2=j2: e.dma_start(out=outs["conv"][j2:j2 + 1, :].rearrange("o (c p) -> p (o c)", p=128),
                                                       in_=uT[:, :, 2 + T - 2 + j2]),
                      uT_b[0], reads=uT_b, is_out=True)
        if smp:
            for cc in range(4):
                for s_ in range(SB):
                    k.dma(sp, lambda e, cc=cc, s_=s_: e.dma_start(
                        out=conv_smp[2 * s_:2 * s_ + 2, cc * 128:(cc + 1) * 128].rearrange("j p -> p j"),
                        in_=uT[:, cc, s_ * 34 + 8:s_ * 34 + 10]),
                          uT_b[cc], reads=[uT_b[cc]], is_out=True)
        if smp:
            sample_attention()
        else:
            prompt_attention(nb, outs["kchunks"])
        pys = {}
        for half in range(2):
            so = load_slab("wout", half * 512, 512)
            for blk in range(nb):
                p = rr("ps", 8)
                pys[(blk, half)] = p

                def f(e, blk=blk, p=p, so=so):
                    last = None
                    for kk in range(8):
                        last = e.matmul(out=PS[p][:], lhsT=catT[:, kk, blk * 128:(blk + 1) * 128], rhs=slab[so][:, kk, :],
                                        start=(kk == 0), stop=(kk == 7))
                    return last
                k.op(pe, f, reads=[slab_b[so]] + catT_b, writes=[PS_b[p]])
        for blk in range(nb):
            post_norm_residual(xi, blk, pys[(blk, 0)], pys[(blk, 1)], "mpost", 1.0)
        ffn(xi, nb, "f2g", "f2u", "f2d", "f2pre", "f2post")
        for (r0, r1, dst) in outs["y"]:
            for blk in range(nb):
                lo, hi = max(r0, blk * 128), min(r1, (blk + 1) * 128)
                if lo >= hi:
                    continue
                k.dma(sp, lambda e, blk=blk, lo=lo, hi=hi: e.dma_start(
                    out=dst[lo - r0:hi - r0, :], in_=xt[xi][lo - blk * 128:hi - blk * 128, blk, :]),
                      xt_b[xi][blk], reads=[xt_b[xi][blk]], is_out=True)

    def prompt_attention(nb, kchunks):
        T = nb * 128
        for h in range(NH):
            accs = {}
            ring["banks"] = [0, 1, 2, 3, 4]
            state["ps"] = 0
            banks = [5, 6, 7]
            for pb in banks:
                k.op(dve, lambda e, pb=pb: e.memset(PS[pb][:], 0.0), writes=[PS_b[pb]])
            idx = 0
            for qb in range(nb):
                for c in range(2):
                    accs[(qb, c)] = (banks[idx // 3], (idx % 3) * 132)
                    idx += 1
            loaded = {}

            def load_chunk(ci):
                if ci >= len(kchunks) or ci in loaded:
                    return
                kp, nkb, kind = kchunks[ci]
                kc = rr("kc", NKC)
                loaded[ci] = kc
                Tk = nkb * 128
                if kind == "meta":
                    k.dma(sp, lambda e: e.dma_start(out=KTc[kc][:, 0:16], in_=KTs[h * 128:(h + 1) * 128, kp:kp + 16]),
                          KTc_b[kc], reads=[KTs_b[kp]], writes=[KTc_b[kc]])
                    k.dma(sp, lambda e: e.dma_start(out=Vc[kc][0:16, 0, 0:128], in_=Vs[kp:kp + 16, h * 128:(h + 1) * 128]),
                          Vc_b[kc], reads=[Vs_b[kp]], writes=[Vc_b[kc]])
                else:
                    k.dma(sp, lambda e: e.dma_start(out=KTc[kc][:, 0:Tk], in_=KTs[h * 128:(h + 1) * 128, kp:kp + Tk]),
                          KTc_b[kc], reads=[KTs_b[kp]], writes=[KTc_b[kc]])
                    k.dma(sp, lambda e: e.dma_start(
                        out=Vc[kc][:, 0:nkb, 0:128], in_=Vs[kp:kp + Tk, h * 128:(h + 1) * 128].rearrange("(b p) c -> p b c", p=128)),
                          Vc_b[kc], reads=[Vs_b[kp]], writes=[Vc_b[kc]])
                if isinstance(kind, tuple):
                    s_ = kind[1]
                    k.op(dve, lambda e: e.tensor_scalar(out=Vc[kc][:, 0:nkb, 0:128], in0=Vc[kc][:, 0:nkb, 0:128],
                                                        scalar1=sel[:, s_:s_ + 1], scalar2=None, op0=ALU.mult),
                         reads=[Vc_b[kc], const_b], writes=[Vc_b[kc]])
                    k.op(dve, lambda e: e.tensor_scalar(out=Vc[kc][:, 0:nkb, 128:129], in0=ones4[:, 0:nkb, :],
                                                        scalar1=sel[:, s_:s_ + 1], scalar2=None, op0=ALU.mult),
                         reads=[Vc_b[kc], const_b], writes=[Vc_b[kc]])
                else:
                    k.op(dve, lambda e: e.tensor_copy(out=Vc[kc][:, 0:nkb, 128:129], in_=ones4[:, 0:nkb, :]),
                         reads=[const_b], writes=[Vc_b[kc]])

            items = [(ci, kb) for ci, (kp, nkb, kind) in enumerate(kchunks) for kb in range(nkb)]

            def emit_qk(it):
                ci, kb = it
                kp, nkb, kind = kchunks[ci]
                if kb == 0:
                    load_chunk(ci)
                    load_chunk(ci + 1)
                kc = loaded[ci]
                KN = 16 if kind == "meta" else 128
                q0 = kb * 128 if kind == "diag" else 0
                N = T - q0
                pts = []
                for c in range(2):
                    p = rr("ps", 8)
                    k.op(pe, lambda e, c=c, p=p: e.matmul(
                        out=PS[p][0:KN, 0:N], lhsT=KTc[kc][c * 64:(c + 1) * 64, kb * 128:kb * 128 + KN],
                        rhs=qT[c * 64:(c + 1) * 64, h, q0:T], start=True, stop=True),
                         reads=[KTc_b[kc], qT_b[h]], writes=[PS_b[p]])
                    t = rr("pt", 4)
                    k.op(actE, lambda e, p=p, t=t: e.activation(out=PT[t][0:KN, 0:N], in_=PS[p][0:KN, 0:N], func=AF.Exp),
                         reads=[PS_b[p]], writes=[PT_b[t]])
                    if kind == "diag":
                        k.op(pool, lambda e, t=t: e.tensor_tensor(out=PT[t][:, 0:128], in0=PT[t][:, 0:128],
                                                                  in1=mask0[:, 0, 0:128], op=ALU.mult),
                             reads=[PT_b[t], const_b], writes=[PT_b[t]])
                    pts.append(t)
                return (kc, kind, kb, q0, KN, pts)

            def emit_av(st):
                kc, kind, kb, q0, KN, pts = st
                for qb in range(q0 // 128, nb):
                    if kind == "diag" and kb > qb:
                        continue
                    for c in range(2):
                        pb, off = accs[(qb, c)]
                        k.op(pe, lambda e, t=pts[c], qb=qb, pb=pb, off=off: e.matmul(
                            out=PS[pb][:, off:off + 129], lhsT=PT[t][0:KN, qb * 128 - q0:(qb + 1) * 128 - q0],
                            rhs=Vc[kc][0:KN, kb, 0:129], start=False, stop=False, skip_group_check=True),
                             reads=[PT_b[pts[c]], Vc_b[kc]], writes=[PS_b[pb]])

            pend = emit_qk(items[0])
            for ii in range(len(items)):
                nxt = emit_qk(items[ii + 1]) if ii + 1 < len(items) else None
                emit_av(pend)
                pend = nxt
            accsb = [(ytmp[0], ytmp_b[0]), (ytmp[1], ytmp_b[1]), (cvt[0], cvt_b[0])]
            for bi, pb in enumerate(banks):
                k.op(dve, lambda e, bi=bi, pb=pb: e.tensor_copy(out=accsb[bi][0][:, 0:396], in_=PS[pb][:, 0:396]),
                     reads=[PS_b[pb]], writes=[accsb[bi][1]])
            for qb in range(nb):
                (pb0, o0), (pb1, o1) = accs[(qb, 0)], accs[(qb, 1)]
                finalize_head(h, qb, (accsb[banks.index(pb0)], o0), (accsb[banks.index(pb1)], o1), 128)
        ring["banks"] = list(range(8))
        state["ps"] = 0

    def finalize_head(h, qb, a0, a1, nrows):
        (pb0, o0), (pb1, o1) = a0, a1
        if isinstance(pb0, int):
            src0, sb0, src1, sb1 = PS[pb0], PS_b[pb0], PS[pb1], PS_b[pb1]
        else:
            (src0, sb0), (src1, sb1) = pb0, pb1
        st, st_b = new_stat()
        R = slice(0, nrows)
        k.op(dve, lambda e: e.tensor_copy(out=st[R, 0:1], in_=src0[R, o0 + 128:o0 + 129]), reads=[sb0], writes=[st_b])
        k.op(dve, lambda e: e.tensor_copy(out=st[R, 1:2], in_=src1[R, o1 + 128:o1 + 129]), reads=[sb1], writes=[st_b])
        k.op(dve, lambda e: e.reciprocal(out=st[R, 0:2], in_=st[R, 0:2]), reads=[st_b], writes=[st_b])
        k.op(dve, lambda e: e.tensor_tensor(out=st[R, 1:2], in0=st[R, 1:2], in1=lamt[R, 4:5], op=ALU.mult),
             reads=[st_b, const_b], writes=[st_b])
        j = rr("osb", 2)
        k.op(dve, lambda e: e.tensor_scalar(out=osb[j][R, :], in0=src0[R, o0:o0 + 128], scalar1=st[R, 0:1], scalar2=None,
                                            op0=ALU.mult), reads=[sb0, st_b], writes=[osb_b[j]])
        k.op(dve, lambda e: e.scalar_tensor_tensor(out=osb[j][R, :], in0=src1[R, o1:o1 + 128], scalar=st[R, 1:2], in1=osb[j][R, :],
                                                   op0=ALU.mult, op1=ALU.add),
             reads=[sb1, st_b, osb_b[j]], writes=[osb_b[j]])
        k.op(actE, lambda e: e.activation(out=junk[R, 0:128], in_=osb[j][R, :], func=AF.Square, accum_out=st[R, 2:3]),
             reads=[osb_b[j]], writes=[junk_b, st_b])
        rstd_from_ssq(st[R, 2:3], st_b, 128, SUBLN_EPS, 1.0, 1, rows=nrows)
        k.op(dve, lambda e: e.scalar_tensor_tensor(out=onb[j][R, :], in0=osb[j][R, :], scalar=st[R, 2:3], in1=subln_g[R, :],
                                                   op0=ALU.mult, op1=ALU.mult),
             reads=[osb_b[j], st_b, const_b], writes=[onb_b[j]])
        return j

    def finalize_prompt(h, qb, j):
        p = rr("ps", 8)
        pbf = PS[p][:].bitcast(BF16)
        k.op(pe, lambda e: e.transpose(out=pbf[:, 0:128], in_=onb[j][:], identity=ident[:]), reads=[onb_b[j], const_b],
             writes=[PS_b[p]])
        k.op(actE, lambda e: e.activation(out=catT[:, h, qb * 128:(qb + 1) * 128], in_=pbf[:, 0:128], func=AF.Copy),
             reads=[PS_b[p]], writes=[catT_b[h]])

    _fh = finalize_head

    def finalize_head(h, qb, a0, a1, nrows):
        j = _fh(h, qb, a0, a1, nrows)
        finalize_prompt(h, qb, j)

    def sample_attention():
        k.op(dve, lambda e: e.tensor_copy(out=Vn[:, :, 0:128], in_=V_sb[:, 0, :].rearrange("p (h c) -> p h c", h=4)),
             reads=[V_sb_b], writes=[Vn_b])
        for h in range(NH):
            k.op(pool, lambda e, h=h: e.memset(catT[:, h, 0:128], 0.0), writes=[catT_b[h]])
        k.op(pool, lambda e: e.memset(Qblk[:], 0.0), writes=[Qblk_b])
        for s_ in range(SB):
            k.op(dve, lambda e, s_=s_: e.tensor_copy(out=Qblk[0:64, s_, :, 0:8], in_=qT[0:64, :, 32 * s_:32 * s_ + 8]),
                 reads=qT_b, writes=[Qblk_b])
            k.op(dve, lambda e, s_=s_: e.tensor_copy(out=Qblk[64:128, s_, :, 8:16], in_=qT[64:128, :, 32 * s_:32 * s_ + 8]),
                 reads=qT_b, writes=[Qblk_b])
        ring["banks"] = [0, 1, 2, 3, 4]
        state["ps"] = 0
        banks = [5, 6, 7]
        for s_ in range(SB):
            for pb in banks:
                k.op(dve, lambda e, pb=pb: e.memset(PS[pb][:], 0.0), writes=[PS_b[pb]])
            accs = {}
            for h in range(NH):
                for c in range(2):
                    i = h * 2 + c
                    accs[(h, c)] = (banks[i // 3], (i % 3) * 132)

            def stage_a(ktile, ktb, mask):
                p = rr("ps", 8)

                def mm(e):
                    last = None
                    for h in range(NH):
                        last = e.matmul(out=PS[p][:, h * 16:(h + 1) * 16], lhsT=ktile[:, h, :], rhs=Qblk[:, s_, h, :],
                                        start=True, stop=True)
                    return last
                k.op(pe, mm, reads=[ktb, Qblk_b], writes=[PS_b[p]])
                t = rr("pts", 2)
                k.op(actE, lambda e: e.activation(out=PTs[t][:], in_=PS[p][:, 0:64], func=AF.Exp), reads=[PS_b[p]], writes=[PTs_b[t]])
                if mask is not None:
                    k.op(dve, lambda e: e.tensor_tensor(out=PTs[t][:], in0=PTs[t][:], in1=mask, op=ALU.mult),
                         reads=[PTs_b[t], const_b], writes=[PTs_b[t]])
                return t

            def stage_b(t, vtile, vtb):
                for h in range(NH):
                    for c in range(2):
                        pb, off = accs[(h, c)]
                        k.op(pe, lambda e, h=h, c=c, pb=pb, off=off: e.matmul(
                            out=PS[pb][0:8, off:off + 129], lhsT=PTs[t][:, h * 16 + c * 8:h * 16 + c * 8 + 8],
                            rhs=vtile[:, h, 0:129], start=False, stop=False, skip_group_check=True),
                             reads=[PTs_b[t], vtb], writes=[PS_b[pb]])

            ck2 = cache_k.ap().rearrange("(r two) c -> r (two c)", two=2)
            cv2 = cache_v.ap().rearrange("(r two) c -> r (two c)", two=2)
            gathered = {}

            def gather(pp):
                if pp >= NPG // 2 or pp in gathered:
                    return
                col = s_ * (NPG // 2) + pp
                gi = rr("kpg", 2)
                gathered[pp] = gi
                k.dma(pool, lambda e: e.indirect_dma_start(
                    out=kpg[gi][:].rearrange("p j c -> p (j c)"), out_offset=None, in_=ck2,
                    in_offset=bass.IndirectOffsetOnAxis(ap=idx[:, col:col + 1], axis=0)),
                      kpg_b[gi], reads=[const_b], writes=[kpg_b[gi]])
                k.dma(pool, lambda e: e.indirect_dma_start(
                    out=vpg[gi][:].rearrange("p j c -> p (j c)"), out_offset=None, in_=cv2,
                    in_offset=bass.IndirectOffsetOnAxis(ap=idx[:, col:col + 1], axis=0)),
                      vpg_b[gi], reads=[const_b], writes=[vpg_b[gi]])

            def front(item):
                if item is None:
                    t = stage_a(KT_sb[:, :, 0:128], KT_sb_b, smask[:, s_, :])
                    return (t, Vn, Vn_b)
                pp, jj = item
                if jj == 0:
                    gather(pp)
                    gather(pp + 1)
                gi = gathered[pp]
                kt = rr("ktp", 2)
                p = rr("ps", 8)

                def tr(e):
                    last = None
                    for h in range(NH):
                        last = e.transpose(out=PS[p][:, h * 128:(h + 1) * 128], in_=kpg[gi][:, jj, h * 128:(h + 1) * 128],
                                           identity=ident_f[:])
                    return last
                k.op(pe, tr, reads=[kpg_b[gi], const_b], writes=[PS_b[p]])
                k.op(actE, lambda e: e.activation(out=KTp[kt][:].rearrange("p h k -> p (h k)"), in_=PS[p][:], func=AF.Copy),
                     reads=[PS_b[p]], writes=[KTp_b[kt]])
                k.op(dve, lambda e: e.tensor_copy(out=Vp[kt][:, :, 0:128],
                                                  in_=vpg[gi][:, jj, :].rearrange("p (h c) -> p h c", h=4)),
                     reads=[vpg_b[gi]], writes=[Vp_b[kt]])
                t = stage_a(KTp[kt], KTp_b[kt], None)
                return (t, Vp[kt], Vp_b[kt])

            sitems = [(pp, jj) for pp in range(NPG // 2) for jj in range(2)] + [None]
            pend = front(sitems[0])
            for ii in range(len(sitems)):
                nxt = front(sitems[ii + 1]) if ii + 1 < len(sitems) else None
                stage_b(*pend)
                pend = nxt
            for h in range(NH):
                j = _fh(h, 0, accs[(h, 0)], accs[(h, 1)], 8)
                p = rr("ps", 8)
                pbf = PS[p][:].bitcast(BF16)
                k.op(pe, lambda e, j=j, pbf=pbf: e.transpose(out=pbf[:, 0:8], in_=onb[j][0:8, :], identity=ident[0:8, 0:8]),
                     reads=[onb_b[j], const_b], writes=[PS_b[p]])
                k.op(actE, lambda e, h=h, pbf=pbf: e.activation(out=catT[:, h, 32 * s_:32 * s_ + 8], in_=pbf[:, 0:8], func=AF.Copy),
                     reads=[PS_b[p]], writes=[catT_b[h]])
        ring["banks"] = list(range(8))
        state["ps"] = 0

    xi = 0
    if cfg.do_sample:
        outs = {"k": k_smp.ap(), "v": v_smp.ap(), "y": [(0, 128, y_smp.ap())]}
        tile_pass(x_smp.ap(), 1, "own", xi, outs=outs, smp=True)
    else:
        raise NotImplementedError("prompt-only build no longer supported (meta keys come from the sample block)")
    for r in range(NR):
        base = r * 2048
        for s in range(1, 4):
            tile_pass(x_pre[(r * 3 + s - 1) * 512:(r * 3 + s) * 512, :], 4, "pre", xi, kpos=base + 512 * s)
            xi ^= 1
        kch = [(cfg.MP, 1, "meta")]
        for rr_ in range(r):
            for s in range(4):
                kch.append((rr_ * 2048 + 512 * s, 4, "full"))
        for s in range(1, 4):
            kch.append((base + 512 * s, 4, ("sel", s)))
        kch.append((base, 4, "diag"))
        outs = {"k": k_own[r * 512:(r + 1) * 512, :], "v": v_own[r * 512:(r + 1) * 512, :],
                "y": [(0, 512, y_own[r * 512:(r + 1) * 512, :])], "kchunks": kch}
        if r == NR - 1:
            outs["conv"] = conv_last.ap()
        tile_pass(x_own[r * 512:(r + 1) * 512, :], 4, "own", xi, kpos=base, outs=outs, halo_cols=2 * r)
        xi ^= 1
    k.finish()
    print("total ops", k.nops, "sems", k.nsem, "cnt pe/act/dve/pool", k.pe.cnt, k.act.cnt, k.dve.cnt, k.pool.cnt)
    return nc, es


def _bf16(a):
    return np.asarray(a, dtype=np.float32).astype(ml_dtypes.bfloat16)


def make_in_maps(cfg, inp):
    NR = cfg.NR
    B = cfg.BATCH
    xp = np.asarray(inp["x_prompt"], dtype=np.float32)
    meta = np.asarray(inp["meta_tokens"], dtype=np.float32)
    maps = []
    kk = np.arange(128)[:, None, None] + 128 * np.arange(4)[None, :, None]
    qq = np.arange(512)[None, None, :]
    dm = (qq >= kk).astype(np.float32).reshape(128, 2048)
    dmask = _bf16(dm)
    common = {}
    for src, dst in [("ffn1_w_gate", "f1g"), ("ffn1_w_up", "f1u"), ("ffn1_w_down", "f1d"), ("w_in", "win"),
                     ("w_out", "wout"), ("ffn2_w_gate", "f2g"), ("ffn2_w_up", "f2u"), ("ffn2_w_down", "f2d")]:
        common[dst] = np.ascontiguousarray(np.asarray(inp[src], dtype=np.float32)[0])
    for src, dst in [("ffn1_pre_g", "f1pre"), ("ffn1_post_g", "f1post"), ("mix_pre_g", "mpre"), ("mix_post_g", "mpost"),
                     ("ffn2_pre_g", "f2pre"), ("ffn2_post_g", "f2post")]:
        common[dst] = np.asarray(inp[src], dtype=np.float32).reshape(1, D)
    common["subln"] = np.asarray(inp["subln_g"], dtype=np.float32).reshape(1, 128)
    common["convw"] = np.ascontiguousarray(np.asarray(inp["conv_w"], dtype=np.float32)[0])
    common["lamv"] = np.concatenate([np.asarray(inp[n], dtype=np.float32).reshape(-1) for n in
                                     ["lambda_q1", "lambda_k1", "lambda_q2", "lambda_k2"]]).reshape(1, 256)
    common["dmask"] = dmask
    rows = np.arange(128)[:, None, None]
    ss = np.arange(4)[None, :, None]
    qcol = (np.arange(64) % 8)[None, None, :]
    t_k = rows - 32 * ss
    sm = ((t_k >= 0) & (t_k < 8) & (t_k <= qcol)).astype(np.float32).reshape(128, 256)
    common["smask"] = _bf16(sm)
    if cfg.do_sample:
        common["cache_k"] = np.asarray(inp["cache_k"], dtype=np.float32).reshape(cfg.NPOOL * 128, 512)
        common["cache_v"] = np.asarray(inp["cache_v"], dtype=np.float32).reshape(cfg.NPOOL * 128, 512)
    xs_all = np.asarray(inp["x_sample"], dtype=np.float32)
    pt_all = np.asarray(inp["page_table"]).astype(np.int32)
    sc_all = np.asarray(inp["state_conv"], dtype=np.float32)[0]
    for c in range(8):
        b, j = c // 4, c % 4
        b = min(b, B - 1)
        seq = xp[b]
        m = dict(common)
        x_pre = np.zeros((NR * 3 * 512, D), np.float32)
        x_own = np.zeros((NR * 512, D), np.float32)
        x_halo = np.zeros((128, D), np.float32)
        sel = np.zeros((128, 4), np.float32)
        for r in range(NR):
            others = [t for t in range(4) if t != j]
            for s, t in enumerate(others):
                g = 4 * r + t
                x_pre[(r * 3 + s) * 512:(r * 3 + s + 1) * 512] = seq[g * 512:(g + 1) * 512]
            g = 4 * r + j
            x_own[r * 512:(r + 1) * 512] = seq[g * 512:(g + 1) * 512]
            if g > 0:
                x_halo[2 * r:2 * r + 2] = seq[g * 512 - 2:g * 512]
            else:
                x_halo[2 * r:2 * r + 2] = meta[N_META - 2:N_META]
        others = [t for t in range(4) if t != j]
        for s, t in enumerate(others):
            sel[:, s + 1] = 1.0 if t < j else 0.0
        x_smp = np.zeros((128, D), np.float32)
        for s_ in range(cfg.SB):
            x_smp[32 * s_:32 * s_ + 8] = xs_all[cfg.SB * c + s_]
        x_smp[8:8 + 2 * NR] = x_halo[0:2 * NR]
        x_smp[40:40 + N_META] = meta
        m.update({"x_pre": x_pre, "x_own": x_own, "x_halo": x_halo, "x_smp": x_smp, "sel": sel})
        if cfg.do_sample:
            m["pt"] = np.ascontiguousarray(pt_all[cfg.SB * c:cfg.SB * (c + 1)]).reshape(1, -1)
            m["sconv"] = np.ascontiguousarray(sc_all[cfg.SB * c:cfg.SB * (c + 1)]).reshape(cfg.SB * 2, 512)
        maps.append(m)
    return maps


def assemble(cfg, res, inp):
    NR, B = cfg.NR, cfg.BATCH
    L = cfg.L
    yp = np.zeros((B, cfg.SEQ, D), np.float32)
    kp = np.zeros((B, L, 512), np.float32)
    vp = np.zeros((B, L, 512), np.float32)
    cp = np.zeros((1, B, 2, 512), np.float32)
    for c in range(8):
        b, j = c // 4, c % 4
        if b >= B:
            continue
        r_ = res[c]
        for r in range(NR):
            g = 4 * r + j
            yp[b, g * 512:(g + 1) * 512] = r_["y_own"][r * 512:(r + 1) * 512]
            kp[b, N_META + g * 512:N_META + (g + 1) * 512] = r_["k_own"][r * 512:(r + 1) * 512]
            vp[b, N_META + g * 512:N_META + (g + 1) * 512] = r_["v_own"][r * 512:(r + 1) * 512]
        if j == 0:
            kp[b, :N_META] = r_["k_smp"][40:40 + N_META]
            vp[b, :N_META] = r_["v_smp"][40:40 + N_META]
        if j == 3:
            cp[0, b] = r_["conv_last"]
    DB = cfg.DEC_BATCH
    ys = np.zeros((DB, 8, D), np.float32)
    ks = np.zeros((1, DB, 8, 512), np.float32)
    vs = np.zeros((1, DB, 8, 512), np.float32)
    cs = np.zeros((1, DB, 2, 512), np.float32)
    for c in range(8):
        for s_ in range(cfg.SB):
            ys[cfg.SB * c + s_] = res[c]["y_smp"][32 * s_:32 * s_ + 8]
            ks[0, cfg.SB * c + s_] = res[c]["k_smp"][32 * s_:32 * s_ + 8]
            vs[0, cfg.SB * c + s_] = res[c]["v_smp"][32 * s_:32 * s_ + 8]
            cs[0, cfg.SB * c + s_] = res[c]["conv_smp"][2 * s_:2 * s_ + 2]
    k_prompt = kp.reshape(1, B, L, 4, 2, 64)
    v_prompt = vp.reshape(1, B, L, 4, 128)
    return (yp, ys, k_prompt, v_prompt, cp, ks.reshape(1, DB, 8, 4, 2, 64), vs.reshape(1, DB, 8, 4, 128), cs)


def run(cfg, inp, trace=False):
    nc, es = build(cfg)
    with es:
        maps = make_in_maps(cfg, inp)
        res = run_bass_kernel_spmd(nc, maps, core_ids=list(range(8)), trace=trace)
    return res


def kernel(**inputs):
    cfg = Cfg()
    res = run(cfg, inputs)
    outs = assemble(cfg, res.results, inputs)
    return outs
```

```python
import numpy as np
import ml_dtypes
from contextlib import ExitStack
import concourse.bass as bass
import concourse.mybir as mybir
from concourse.bass_utils import run_bass_kernel_spmd

F32 = mybir.dt.float32
BF16 = mybir.dt.bfloat16
I32 = mybir.dt.int32
AF = mybir.ActivationFunctionType
ALU = mybir.AluOpType
IndOff = bass.IndirectOffsetOnAxis if hasattr(bass, "IndirectOffsetOnAxis") else None

D = 1024
DFF = 2816
NF = 22
NH = 4
LAMBDA_INIT = 0.8 - 0.6 * 1.0
RMS_EPS = 1e-6
SUBLN_EPS = 1e-5
N_META = 16


class Cfg:
    def __init__(self, SEQ=8192, PAST=8192, DEC_BATCH=32, BATCH=2, do_sample=True):
        self.SEQ, self.PAST, self.DEC_BATCH, self.BATCH = SEQ, PAST, DEC_BATCH, BATCH
        self.L = SEQ + N_META
        assert SEQ % 2048 == 0
        self.NR = SEQ // 2048
        self.NPG = PAST // 128
        self.SB = DEC_BATCH // 8
        self.NPOOL = (DEC_BATCH * self.NPG) + (DEC_BATCH * self.NPG) // 4
        self.LP = self.NR * 2048 + 128
        self.MP = self.NR * 2048
        self.do_sample = do_sample


class Buf:
    __slots__ = ("name", "w", "r", "dsem", "dcnt", "excl")

    def __init__(self, name, excl=False):
        self.name = name
        self.excl = excl
        self.w = None
        self.r = {}
        self.dsem = None
        self.dcnt = 0


class Eng:
    def __init__(self, h, sem, is_pe=False):
        self.h = h
        self.sem = sem
        self.cnt = 0
        self.waited = {}
        self.is_pe = is_pe


class K:
    def __init__(self, nc, es):
        self.nc = nc
        self.es = es
        self.nsem = 0
        self.pe = Eng(nc.tensor, self.newsem("pe"), True)
        self.act = Eng(nc.scalar, self.newsem("act"))
        self.dve = Eng(nc.vector, self.newsem("dve"))
        self.pool = Eng(nc.gpsimd, self.newsem("pool"))
        self.sp = Eng(nc.sync, self.newsem("sp"))
        self.out_tokens = []
        self.nops = 0
        import os as _os
        self.max_ops = int(_os.environ.get("KMAXOPS", "1000000000"))
        self.dbg = bool(_os.environ.get("KDBG"))

    def newsem(self, name):
        self.nsem += 1
        return self.es.enter_context(self.nc.semaphore(f"s_{name}_{self.nsem}"))

    def _deps(self, eng, reads, writes):
        deps = {}

        def add(tok):
            if tok is None:
                return
            s, v = tok
            k = id(s)
            if k not in deps or deps[k][1] < v:
                deps[k] = (s, v)

        for b in reads:
            add(b.w)
            if b.excl:
                for t in b.r.values():
                    add(t)
        for b in writes:
            add(b.w)
            for t in b.r.values():
                add(t)
        for k, (s, v) in deps.items():
            if eng.is_pe and s is eng.sem:
                continue
            if eng.waited.get(k, 0) >= v:
                continue
            eng.h.wait_ge(s, v)
            eng.waited[k] = v

    def _commit(self, tok, reads, writes):
        for b in writes:
            b.w = tok
            b.r = {}
        for b in reads:
            b.r[id(tok[0])] = tok

    def op(self, eng, fn, reads=(), writes=()):
        self.nops += 1
        if self.nops > self.max_ops:
            return None
        if self.dbg:
            print("OP", self.nops, fn.__code__.co_firstlineno, [b.name for b in reads], [b.name for b in writes])
        self._deps(eng, reads, writes)
        inst = fn(eng.h)
        eng.cnt += 1
        inst.then_inc(eng.sem, 1)
        tok = (eng.sem, eng.cnt)
        self._commit(tok, reads, writes)
        return tok

    def dma(self, eng, fn, sb, reads=(), writes=(), is_out=False):
        self.nops += 1
        if self.nops > self.max_ops:
            return None
        if self.dbg:
            print("DMA", self.nops, fn.__code__.co_firstlineno, [b.name for b in reads], [b.name for b in writes])
        self._deps(eng, reads, writes)
        if sb.dsem is None:
            sb.dsem = self.newsem("d_" + sb.name)
        inst = fn(eng.h)
        sb.dcnt += 16
        inst.then_inc(sb.dsem, 16)
        tok = (sb.dsem, sb.dcnt)
        self._commit(tok, reads, writes)
        if is_out:
            self.out_tokens.append(tok)
        return tok

    def finish(self):
        last = {}
        for s, v in self.out_tokens:
            if id(s) not in last or last[id(s)][1] < v:
                last[id(s)] = (s, v)
        for s, v in last.values():
            self.sp.h.wait_ge(s, v)


def build(cfg):
    nc = bass.Bass("TRN2", target_bir_lowering=False)
    es = ExitStack()
    k = K(nc, es)
    NR, NPG, SB, LP = cfg.NR, cfg.NPG, cfg.SB, cfg.LP

    def din(name, shape, dt=F32):
        return nc.dram_tensor(name, list(shape), dt, kind="ExternalInput")

    def dout(name, shape, dt=F32):
        return nc.dram_tensor(name, list(shape), dt, kind="ExternalOutput")

    x_pre = din("x_pre", [NR * 3 * 512, D])
    x_own = din("x_own", [NR * 512, D])
    x_halo = din("x_halo", [128, D])
    x_smp = din("x_smp", [128, D])
    dmask = din("dmask", [128, 4 * 512], BF16)
    sel_in = din("sel", [128, 4])
    smask_in = din("smask", [128, 4 * 64], BF16)
    W = {}
    for nm, shp in [("f1g", [D, DFF]), ("f1u", [D, DFF]), ("f1d", [DFF, D]), ("win", [D, 3072]),
                    ("wout", [D, D]), ("f2g", [D, DFF]), ("f2u", [D, DFF]), ("f2d", [DFF, D])]:
        W[nm] = din(nm, shp)
    V = {}
    for nm in ["f1pre", "f1post", "mpre", "mpost", "f2pre", "f2post"]:
        V[nm] = din(nm, [1, D])
    subln_in = din("subln", [1, 128])
    convw_in = din("convw", [3, 512])
    lam_in = din("lamv", [1, 256])
    if cfg.do_sample:
        cache_k = din("cache_k", [cfg.NPOOL * 128, 512])
        cache_v = din("cache_v", [cfg.NPOOL * 128, 512])
        pt_in = din("pt", [1, SB * NPG], I32)
        sconv_in = din("sconv", [SB * 2, 512])

    y_own = dout("y_own", [NR * 512, D])
    k_own = dout("k_own", [NR * 512, 512])
    v_own = dout("v_own", [NR * 512, 512])
    conv_last = dout("conv_last", [2, 512])
    y_smp = dout("y_smp", [128, D])
    k_smp = dout("k_smp", [128, 512])
    v_smp = dout("v_smp", [128, 512])
    conv_smp = dout("conv_smp", [SB * 2, 512])

    KTs = nc.dram_tensor("KTs", [512, LP], BF16)
    Vs = nc.dram_tensor("Vs", [LP, 512], BF16)
    KTs_b = {}
    Vs_b = {}

    def sb(name, shape, dt):
        return nc.alloc_sbuf_tensor("sb_" + name, list(shape), dt)

    xt = [sb("xt0", [128, 4, D], F32)] * 2
    xt_b = [[Buf(f"xt0_{j}") for j in range(4)]] * 2
    junk = sb("junk", [128, D], BF16)
    junk_b = Buf("junk")
    xs = [sb(f"xs{i}", [128, D], BF16) for i in range(2)]
    xs_b = [Buf(f"xs{i}") for i in range(2)]
    xnT = sb("xnT", [128, 8, 512], BF16)
    xnT_b = [Buf(f"xnT{i}") for i in range(8)]
    act = sb("act", [128, NF, 512], BF16)
    act_b = [Buf(f"act{i}") for i in range(NF)]
    sg = [sb(f"sg{i}", [128, 512], BF16) for i in range(2)]
    sg_b = [Buf(f"sg{i}") for i in range(2)]
    NSLAB = 4
    slab2 = [sb(f"slabp{i}", [128, 2, 8, 512], BF16) for i in range(NSLAB // 2)]
    slab = [slab2[i // 2][:, i % 2] for i in range(NSLAB)]
    slab_b = [Buf(f"slab{i}") for i in range(NSLAB)]
    wd = [sb(f"wd{i}", [128, 5, D], BF16) for i in range(2)]
    wd_b = [Buf(f"wd{i}") for i in range(2)]
    ytmp = [sb(f"ytmp{i}", [128, 512], F32) for i in range(2)]
    ytmp_b = [Buf(f"ytmp{i}") for i in range(2)]
    gpost = {nm: sb("gb_" + nm, [128, D], F32) for nm in ["f1post", "mpost", "f2post"]}
    gcol = {nm: sb("gc_" + nm, [128, 8], F32) for nm in ["f1pre", "mpre", "f2pre"]}
    const_b = Buf("consts")
    stat = sb("stat", [128, 64], F32)
    stat_b = [Buf(f"stat{i}") for i in range(16)]
    KT_sb = sb("KT_sb", [128, 4, 512], BF16)
    KT_sb_b = Buf("KT_sb")
    V_sb = sb("V_sb", [128, 4, 512], BF16)
    V_sb_b = Buf("V_sb")
    kf = [sb(f"kf{i}", [128, 512], F32) for i in range(2)]
    kf_b = [Buf(f"kf{i}") for i in range(2)]
    qT = sb("qT", [128, 4, 512], BF16)
    qT_b = [Buf(f"qT{i}") for i in range(4)]
    uT = sb("uT", [128, 4, 516], F32)
    uT_b = [Buf(f"uT{i}") for i in range(4)]
    gcs = [sb(f"gcs{i}", [128, 512], F32) for i in range(2)]
    gcs_b = [Buf(f"gcs{i}") for i in range(2)]
    cvt = [sb(f"cvt{i}", [128, 512], F32) for i in range(2)]
    cvt_b = [Buf(f"cvt{i}") for i in range(2)]
    catT = sb("catT", [128, 8, 512], BF16)
    catT_b = [Buf(f"catT{i}") for i in range(8)]
    halo_u = sb("halo_u", [128, 4, 16 if cfg.do_sample else 128], F32)
    halo_b = Buf("halo_u")
    convw = sb("convw", [128, 4, 3], F32)
    subln_g = sb("subln_g", [128, 128], F32)
    lamw = sb("lamw", [128, 256], F32)
    lamt = sb("lamt", [128, 8], F32)
    sel = sb("sel", [128, 4], F32)
    mask0 = sb("mask0", [128, 4, 512], BF16)
    ident = sb("ident", [128, 128], BF16)
    epst = sb("epst", [128, 2], F32)
    ones_bf = sb("ones_bf", [128, 128], BF16)
    ones4 = sb("ones4", [128, 4, 1], BF16)
    ident_f = sb("ident_f", [128, 128], F32)
    NKC = 3
    KTc = [sb(f"KTc{i}", [128, 512], BF16) for i in range(NKC)]
    KTc_b = [Buf(f"KTc{i}") for i in range(NKC)]
    Vc = [sb(f"Vc{i}", [128, 4, 132], BF16) for i in range(NKC)]
    Vc_b = [Buf(f"Vc{i}") for i in range(NKC)]
    PT = [sb(f"PT{i}", [128, 512], BF16) for i in range(4)]
    PT_b = [Buf(f"PT{i}") for i in range(4)]
    osb = [sb(f"osb{i}", [128, 128], F32) for i in range(2)]
    osb_b = [Buf(f"osb{i}") for i in range(2)]
    onb = [sb(f"onb{i}", [128, 128], BF16) for i in range(2)]
    onb_b = [Buf(f"onb{i}") for i in range(2)]

    if cfg.do_sample:
        NPP = SB * NPG // 2
        pti = sb("pti", [128, NPP], I32)
        idx = sb("idx", [128, NPP], I32)
        iota_f = sb("iota_f", [128, 1], F32)
        smask = sb("smask", [128, 4, 64], BF16)
        kpg = [sb(f"kpg{i}", [128, 2, 512], F32) for i in range(2)]
        kpg_b = [Buf(f"kpg{i}") for i in range(2)]
        vpg = [sb(f"vpg{i}", [128, 2, 512], F32) for i in range(2)]
        vpg_b = [Buf(f"vpg{i}") for i in range(2)]
        KTp = [sb(f"KTp{i}", [128, 4, 128], BF16) for i in range(2)]
        KTp_b = [Buf(f"KTp{i}") for i in range(2)]
        Vp = [sb(f"Vp{i}", [128, 4, 132], BF16) for i in range(2)]
        Vp_b = [Buf(f"Vp{i}") for i in range(2)]
        Vn = sb("Vn", [128, 4, 132], BF16)
        Vn_b = Buf("Vn")
        PTs = [sb(f"PTs{i}", [128, 64], BF16) for i in range(2)]
        PTs_b = [Buf(f"PTs{i}") for i in range(2)]
        Qblk = sb("Qblk", [128, 4, 4, 16], BF16)
        Qblk_b = Buf("Qblk")
    print("sbuf bytes remaining", nc.sbuf_bytes_remaining)
    PS = [nc.alloc_psum_tensor(f"ps{i}", [128, 512], F32) for i in range(8)]
    PS_b = [Buf(f"ps{i}", excl=True) for i in range(8)]

    pe, actE, dve, pool, sp = k.pe, k.act, k.dve, k.pool, k.sp
    nc_allow = nc.allow_non_contiguous_dma(reason="tiny constant / transposed vector loads")
    es.enter_context(nc_allow)
    es.enter_context(nc.allow_low_precision(reason="bf16 matmul operands, fp32 accumulate"))

    for nm in gpost:
        k.dma(sp, lambda e, nm=nm: e.dma_start(out=gpost[nm][:], in_=V[nm][0:1, :].broadcast_to([128, D])),
              const_b, writes=[const_b])
    for nm in gcol:
        k.dma(sp, lambda e, nm=nm: e.dma_start(out=gcol[nm][:], in_=V[nm].ap().rearrange("o (k p) -> p (o k)", p=128)),
              const_b, writes=[const_b])
    for j3 in range(3):
        k.dma(sp, lambda e, j3=j3: e.dma_start(out=convw[:, :, j3], in_=convw_in[j3:j3 + 1, :].rearrange("o (c p) -> p (o c)", p=128)),
              const_b, writes=[const_b])
    k.dma(sp, lambda e: e.dma_start(out=subln_g[:], in_=subln_in[0:1, :].broadcast_to([128, 128])),
          const_b, writes=[const_b])
    k.dma(sp, lambda e: e.dma_start(out=lamw[:], in_=lam_in[0:1, :].broadcast_to([128, 256])),
          const_b, writes=[const_b])
    k.dma(sp, lambda e: e.dma_start(out=sel[:], in_=sel_in.ap()), const_b, writes=[const_b])
    k.dma(sp, lambda e: e.dma_start(out=mask0[:], in_=dmask.ap().rearrange("p (a q) -> p a q", a=4)),
          const_b, writes=[const_b])
    if cfg.do_sample:
        k.dma(sp, lambda e: e.dma_start(out=smask[:], in_=smask_in.ap().rearrange("p (a q) -> p a q", a=4)),
              const_b, writes=[const_b])
        ptv = pt_in.ap().rearrange("o (n two) -> o n two", two=2)
        k.dma(sp, lambda e: e.dma_start(out=pti[0:64, :], in_=ptv[0:1, :, 0].broadcast_to([64, NPP])), const_b, writes=[const_b])
        k.dma(sp, lambda e: e.dma_start(out=pti[64:128, :], in_=ptv[0:1, :, 1].broadcast_to([64, NPP])), const_b, writes=[const_b])
        k.op(pool, lambda e: e.iota(out=iota_f[0:64, :], pattern=[[0, 1]], base=0, channel_multiplier=1,
                                    allow_small_or_imprecise_dtypes=True), writes=[const_b])
        k.op(pool, lambda e: e.iota(out=iota_f[64:128, :], pattern=[[0, 1]], base=0, channel_multiplier=1,
                                    allow_small_or_imprecise_dtypes=True), writes=[const_b])
        k.op(dve, lambda e: e.tensor_scalar(out=idx[:], in0=pti[:], scalar1=64.0, scalar2=iota_f[:, 0:1],
                                            op0=ALU.mult, op1=ALU.add), reads=[const_b], writes=[const_b])
        for i in range(2):
            k.op(pool, lambda e, i=i: e.memset(Vp[i][:], 1.0), writes=[Vp_b[i]])
        k.op(pool, lambda e: e.memset(Vn[:], 1.0), writes=[Vn_b])
    k.op(pool, lambda e: e.memset(epst[:, 0:1], RMS_EPS), writes=[const_b])
    k.op(pool, lambda e: e.memset(epst[:, 1:2], SUBLN_EPS), writes=[const_b])
    for nm in ("f1post", "f2post"):
        k.op(dve, lambda e, nm=nm: e.tensor_scalar(out=gpost[nm][:], in0=gpost[nm][:], scalar1=0.5, scalar2=None, op0=ALU.mult),
             reads=[const_b], writes=[const_b])
    k.op(pool, lambda e: e.memset(ones_bf[:], 1.0), writes=[const_b])
    k.op(pool, lambda e: e.memset(ones4[:], 1.0), writes=[const_b])
    k.op(pool, lambda e: e.affine_select(out=ident[:], in_=ones_bf[:], pattern=[[1, 128]],
                                         compare_op=ALU.is_equal, fill=0.0, base=0, channel_multiplier=-1),
         reads=[const_b], writes=[const_b])
    k.op(dve, lambda e: e.tensor_copy(out=ident_f[:], in_=ident[:]), reads=[const_b], writes=[const_b])
    for i in range(NKC):
        k.op(pool, lambda e, i=i: e.memset(Vc[i][:], 1.0), writes=[Vc_b[i]])
    k.op(dve, lambda e: e.scalar_tensor_tensor(out=lamw[:, 0:64], in0=lamw[:, 0:64], scalar=1.0, in1=lamw[:, 64:128],
                                               op0=ALU.mult, op1=ALU.mult, accum_out=lamt[:, 0:1]),
         reads=[const_b], writes=[const_b])
    k.op(dve, lambda e: e.scalar_tensor_tensor(out=lamw[:, 128:192], in0=lamw[:, 128:192], scalar=1.0, in1=lamw[:, 192:256],
                                               op0=ALU.mult, op1=ALU.mult, accum_out=lamt[:, 1:2]),
         reads=[const_b], writes=[const_b])
    k.op(actE, lambda e: e.activation(out=lamt[:, 2:4], in_=lamt[:, 0:2], func=AF.Exp), reads=[const_b], writes=[const_b])
    k.op(dve, lambda e: e.tensor_tensor(out=lamt[:, 4:5], in0=lamt[:, 3:4], in1=lamt[:, 2:3], op=ALU.subtract),
         reads=[const_b], writes=[const_b])
    k.op(dve, lambda e: e.tensor_scalar(out=lamt[:, 4:5], in0=lamt[:, 4:5], scalar1=-LAMBDA_INIT, scalar2=None, op0=ALU.add),
         reads=[const_b], writes=[const_b])
    k.op(dve, lambda e: e.tensor_scalar(out=subln_g[:], in0=subln_g[:], scalar1=1.0 - LAMBDA_INIT, scalar2=None, op0=ALU.mult),
         reads=[const_b], writes=[const_b])

    state = {"slab": 0, "wd": 0, "ps": 0, "stat": 0, "sg": 0, "ytmp": 0, "xs": 0, "kf": 0, "gcs": 0, "cvt": 0,
             "kc": 0, "pt": 0, "osb": 0, "pts": 0, "kpg": 0, "ktp": 0}

    ring = {"banks": list(range(8))}

    def rr(key, n):
        v = state[key]
        if key == "ps":
            b = ring["banks"]
            state[key] = (v + 1) % len(b)
            return b[v % len(b)]
        state[key] = (v + 1) % n
        return v

    wscr = {}

    def load_slab(wname, col0, ncols):
        i = rr("slab", NSLAB)
        key = (wname, col0)
        if key not in wscr:
            src = W[wname][:, col0:col0 + ncols].rearrange("(k p) c -> p k c", p=128)
            k.dma(pool, lambda e: e.dma_start(out=slab[i][:, :, 0:ncols], in_=src), slab_b[i], writes=[slab_b[i]])
            scr = nc.dram_tensor(f"ws_{wname}_{col0}", [128, 8 * ncols], BF16)
            sbuf_ = Buf(f"ws_{wname}_{col0}")
            wscr[key] = (scr, sbuf_)
            k.dma(sp, lambda e: e.dma_start(out=scr.ap().rearrange("p (k c) -> p k c", k=8), in_=slab[i][:, :, 0:ncols]),
                  sbuf_, reads=[slab_b[i]], writes=[sbuf_])
        else:
            scr, sbuf_ = wscr[key]
            k.dma(pool, lambda e: e.dma_start(out=slab[i][:, :, 0:ncols], in_=scr.ap().rearrange("p (k c) -> p k c", k=8)),
                  slab_b[i], reads=[sbuf_], writes=[slab_b[i]])
        return i

    def load_slab_pair(wa, wb, col0, ncols):
        if state["slab"] % 2 == 1:
            rr("slab", NSLAB)
        ia = rr("slab", NSLAB)
        ib = rr("slab", NSLAB)
        pi = ia // 2
        key = (wa, wb, col0)
        if key not in wscr:
            for (wn, ii) in ((wa, ia), (wb, ib)):
                src = W[wn][:, col0:col0 + ncols].rearrange("(k p) c -> p k c", p=128)
                k.dma(pool, lambda e, ii=ii, src=src: e.dma_start(out=slab[ii][:, :, 0:ncols], in_=src), slab_b[ii], writes=[slab_b[ii]])
            scr = nc.dram_tensor(f"wsp_{wa}_{col0}", [128, 2 * 8 * ncols], BF16)
            sbuf_ = Buf(f"wsp_{wa}_{col0}")
            wscr[key] = (scr, sbuf_)
            k.dma(sp, lambda e: e.dma_start(out=scr.ap().rearrange("p (t k c) -> p t k c", t=2, k=8), in_=slab2[pi][:, :, :, 0:ncols]),
                  sbuf_, reads=[slab_b[ia], slab_b[ib]], writes=[sbuf_])
        else:
            scr, sbuf_ = wscr[key]
            k.dma(pool, lambda e: e.dma_start(out=slab2[pi][:, :, :, 0:ncols], in_=scr.ap().rearrange("p (t k c) -> p t k c", t=2, k=8)),
                  slab_b[ia], reads=[sbuf_], writes=[slab_b[ia], slab_b[ib]])
        return ia, ib

    def load_wd(wname, f0, f1):
        i = rr("wd", 2)
        nf = f1 - f0
        key = (wname, f0)
        if key not in wscr:
            src = W[wname][f0 * 128:f1 * 128, :].rearrange("(f p) c -> p f c", p=128)
            k.dma(pool, lambda e: e.dma_start(out=wd[i][:, 0:nf, :], in_=src), wd_b[i], writes=[wd_b[i]])
            scr = nc.dram_tensor(f"ws_{wname}_{f0}", [128, nf * D], BF16)
            sbuf_ = Buf(f"ws_{wname}_{f0}")
            wscr[key] = (scr, sbuf_)
            k.dma(sp, lambda e: e.dma_start(out=scr.ap().rearrange("p (f c) -> p f c", f=nf), in_=wd[i][:, 0:nf, :]),
                  sbuf_, reads=[wd_b[i]], writes=[sbuf_])
        else:
            scr, sbuf_ = wscr[key]
            k.dma(pool, lambda e: e.dma_start(out=wd[i][:, 0:nf, :], in_=scr.ap().rearrange("p (f c) -> p f c", f=nf)),
                  wd_b[i], reads=[sbuf_], writes=[wd_b[i]])
        return i

    def new_stat():
        i = rr("stat", 16)
        return stat[:, 4 * i:4 * i + 4], stat_b[i]

    def rstd_from_ssq(ssq_ap, ssq_buf, n, eps, mul, ncols, rows=128):
        ecol = 0 if eps == RMS_EPS else 1
        k.op(actE, lambda e: e.activation(out=ssq_ap, in_=ssq_ap, func=AF.Sqrt, bias=epst[0:rows, ecol:ecol + 1],
                                          scale=1.0 / n), reads=[ssq_buf, const_b], writes=[ssq_buf])
        k.op(dve, lambda e: e.reciprocal(out=ssq_ap, in_=ssq_ap), reads=[ssq_buf], writes=[ssq_buf])

    def norm_transpose(xi, nb, gname):
        T = nb * 128
        st, st_b = new_stat()
        for blk in range(nb):
            k.op(actE, lambda e, blk=blk: e.activation(out=junk[:], in_=xt[xi][:, blk, :], func=AF.Square,
                                                       accum_out=st[:, blk:blk + 1]),
                 reads=[xt_b[xi][blk]], writes=[junk_b, st_b])
        rstd_from_ssq(st[:, 0:nb], st_b, D, RMS_EPS, 1.0, nb)
        for blk in range(nb):
            j = blk % 2
            k.op(actE, lambda e, blk=blk, j=j: e.activation(out=xs[j][:], in_=xt[xi][:, blk, :], func=AF.Copy,
                                                            scale=st[:, blk:blk + 1]),
                 reads=[xt_b[xi][blk], st_b], writes=[xs_b[j]])
            for half in range(2):
                p = rr("ps", 8)
                pbf = PS[p][:].bitcast(BF16)

                def tr(e, j=j, half=half, pbf=pbf):
                    last = None
                    for q in range(4):
                        kk = half * 4 + q
                        last = e.transpose(out=pbf[:, q * 128:(q + 1) * 128], in_=xs[j][:, kk * 128:(kk + 1) * 128],
                                           identity=ident[:])
                    return last
                k.op(pe, tr, reads=[xs_b[j], const_b], writes=[PS_b[p]])
                gb_ = gcol[gname][:, half * 4:half * 4 + 4].unsqueeze(2).broadcast_to([128, 4, 128])
                k.op(dve, lambda e, blk=blk, half=half, pbf=pbf, gb_=gb_: e.tensor_tensor(
                    out=xnT[:, half * 4:half * 4 + 4, blk * 128:(blk + 1) * 128],
                    in0=pbf[:, 0:512].rearrange("p (q c) -> p q c", q=4), in1=gb_, op=ALU.mult),
                     reads=[PS_b[p], const_b], writes=xnT_b[half * 4:half * 4 + 4])

    def post_norm_residual(xi, blk, pA, pB, gname, mul):
        st, st_b = new_stat()
        for h2, p in enumerate((pA, pB)):
            k.op(actE, lambda e, h2=h2, p=p: e.activation(out=junk[:, 0:512], in_=PS[p][:], func=AF.Square,
                                                          accum_out=st[:, h2:h2 + 1]),
                 reads=[PS_b[p]], writes=[junk_b, st_b])
        k.op(dve, lambda e: e.tensor_tensor(out=st[:, 2:3], in0=st[:, 0:1], in1=st[:, 1:2], op=ALU.add),
             reads=[st_b], writes=[st_b])
        rstd_from_ssq(st[:, 2:3], st_b, D, RMS_EPS, mul, 1)
        for h2, p in enumerate((pA, pB)):
            t = rr("ytmp", 2)
            k.op(dve, lambda e, h2=h2, p=p, t=t: e.tensor_tensor(out=ytmp[t][:], in0=PS[p][:],
                                                                 in1=gpost[gname][:, h2 * 512:(h2 + 1) * 512], op=ALU.mult),
                 reads=[PS_b[p], const_b], writes=[ytmp_b[t]])
            k.op(dve, lambda e, h2=h2, t=t: e.scalar_tensor_tensor(
                out=xt[xi][:, blk, h2 * 512:(h2 + 1) * 512], in0=ytmp[t][:], scalar=st[:, 2:3],
                in1=xt[xi][:, blk, h2 * 512:(h2 + 1) * 512], op0=ALU.mult, op1=ALU.add),
                 reads=[ytmp_b[t], st_b, xt_b[xi][blk]], writes=[xt_b[xi][blk]])

    def ffn(xi, nb, wg, wu, wdn, gpre, gpostn):
        T = nb * 128
        norm_transpose(xi, nb, gpre)
        for fg in range(6):
            nf = 4 if fg < 5 else 2
            sgi, sui = load_slab_pair(wg, wu, fg * 512, nf * 128)
            for fi in range(nf):
                f = fg * 4 + fi
                pg = rr("ps", 8)
                pu = rr("ps", 8)

                def mm(e, si, p, fi=fi):
                    last = None
                    for kk in range(8):
                        last = e.matmul(out=PS[p][:, 0:T], lhsT=slab[si][:, kk, fi * 128:(fi + 1) * 128],
                                        rhs=xnT[:, kk, 0:T], start=(kk == 0), stop=(kk == 7))
                    return last
                k.op(pe, lambda e, p=pg: mm(e, sgi, p), reads=[slab_b[sgi]] + xnT_b, writes=[PS_b[pg]])
                k.op(pe, lambda e, p=pu: mm(e, sui, p), reads=[slab_b[sui]] + xnT_b, writes=[PS_b[pu]])
                s = rr("sg", 2)
                k.op(actE, lambda e, s=s, p=pg: e.activation(out=sg[s][:, 0:T], in_=PS[p][:, 0:T], func=AF.Silu),
                     reads=[PS_b[pg]], writes=[sg_b[s]])
                k.op(dve, lambda e, s=s, p=pu, f=f: e.tensor_tensor(out=act[:, f, 0:T], in0=sg[s][:, 0:T],
                                                                    in1=PS[p][:, 0:T], op=ALU.mult),
                     reads=[sg_b[s], PS_b[pu]], writes=[act_b[f]])
        pys = {}
        for blk in range(nb):
            for half in range(2):
                pys[(blk, half)] = rr("ps", 8)
        for (f0, f1) in [(0, 5), (5, 10), (10, 14), (14, 18), (18, 22)]:
            wi = load_wd(wdn, f0, f1)
            for blk in range(nb):
                for half in range(2):
                    p = pys[(blk, half)]

                    def mmd(e, blk=blk, half=half, p=p, wi=wi, f0=f0, f1=f1):
                        last = None
                        for f in range(f0, f1):
                            last = e.matmul(out=PS[p][:], lhsT=act[:, f, blk * 128:(blk + 1) * 128],
                                            rhs=wd[wi][:, f - f0, half * 512:(half + 1) * 512],
                                            start=(f == 0), stop=(f == NF - 1))
                        return last
                    k.op(pe, mmd, reads=[wd_b[wi]] + act_b[f0:f1], writes=[PS_b[p]])
        for blk in range(nb):
            post_norm_residual(xi, blk, pys[(blk, 0)], pys[(blk, 1)], gpostn, 0.5)
        return

    def mm_fm(si, c0, p, T):
        def f(e):
            last = None
            for kk in range(8):
                last = e.matmul(out=PS[p][:, 0:T], lhsT=slab[si][:, kk, c0:c0 + 128], rhs=xnT[:, kk, 0:T],
                                start=(kk == 0), stop=(kk == 7))
            return last
        k.op(pe, f, reads=[slab_b[si]] + xnT_b, writes=[PS_b[p]])

    def mm_tm(si, blk, p):
        def f(e):
            last = None
            for kk in range(8):
                last = e.matmul(out=PS[p][:], lhsT=xnT[:, kk, blk * 128:(blk + 1) * 128], rhs=slab[si][:, kk, :],
                                start=(kk == 0), stop=(kk == 7))
            return last
        k.op(pe, f, reads=[slab_b[si]] + xnT_b, writes=[PS_b[p]])

    def tile_pass(xsrc, nb, mode, xi, kpos=None, outs=None, halo_cols=None, smp=False):
        T = nb * 128
        blks = [xt_b[xi][j] for j in range(nb)]
        k.dma(sp, lambda e: e.dma_start(out=xt[xi][:, 0:nb, :], in_=xsrc.rearrange("(b p) d -> p b d", p=128)),
              blks[0], writes=blks)
        ffn(xi, nb, "f1g", "f1u", "f1d", "f1pre", "f1post")
        norm_transpose(xi, nb, "mpre")
        own = mode == "own"
        if mode in ("pre", "own"):
            sk = load_slab("win", 512, 512)
            for h in range(NH):
                p = rr("ps", 8)
                mm_fm(sk, h * 128, p, T)
                eng = dve if h % 2 == 0 else actE
                if eng is dve:
                    k.op(dve, lambda e, h=h, p=p: e.tensor_copy(out=KT_sb[:, h, 0:T], in_=PS[p][:, 0:T]),
                         reads=[PS_b[p]], writes=[KT_sb_b])
                else:
                    k.op(actE, lambda e, h=h, p=p: e.activation(out=KT_sb[:, h, 0:T], in_=PS[p][:, 0:T], func=AF.Copy),
                         reads=[PS_b[p]], writes=[KT_sb_b])
            if not smp:
                rb = Buf(f"KTs_{kpos}")
                KTs_b[kpos] = rb
                k.dma(sp, lambda e: e.dma_start(out=KTs[:, kpos:kpos + T].rearrange("(h p) t -> p h t", p=128),
                                                in_=KT_sb[:, :, 0:T]), KT_sb_b, reads=[KT_sb_b], writes=[rb])
            else:
                rb = Buf("KTs_meta")
                KTs_b[cfg.MP] = rb
                k.dma(sp, lambda e: e.dma_start(out=KTs[:, cfg.MP:cfg.MP + 16].rearrange("(h p) t -> p h t", p=128),
                                                in_=KT_sb[:, :, 40:56]), KT_sb_b, reads=[KT_sb_b], writes=[rb])
            if own:
                for blk in range(nb):
                    p = rr("ps", 8)
                    mm_tm(sk, blk, p)
                    j = rr("kf", 2)
                    k.op(actE, lambda e, p=p, j=j: e.activation(out=kf[j][:], in_=PS[p][:], func=AF.Copy),
                         reads=[PS_b[p]], writes=[kf_b[j]])
                    k.dma(sp, lambda e, blk=blk, j=j: e.dma_start(out=outs["k"][blk * 128:(blk + 1) * 128, :], in_=kf[j][:]),
                          kf_b[j], reads=[kf_b[j]], is_out=True)
            sv = load_slab("win", 1024, 512)
            for blk in range(nb):
                p = rr("ps", 8)
                mm_tm(sv, blk, p)
                k.op(dve, lambda e, blk=blk, p=p: e.tensor_copy(out=V_sb[:, blk, :], in_=PS[p][:]),
                     reads=[PS_b[p]], writes=[V_sb_b])
                if own:
                    j = rr("kf", 2)
                    k.op(actE, lambda e, p=p, j=j: e.activation(out=kf[j][:], in_=PS[p][:], func=AF.Copy),
                         reads=[PS_b[p]], writes=[kf_b[j]])
                    k.dma(sp, lambda e, blk=blk, j=j: e.dma_start(out=outs["v"][blk * 128:(blk + 1) * 128, :], in_=kf[j][:]),
                          kf_b[j], reads=[kf_b[j]], is_out=True)
            if not smp:
                rb = Buf(f"Vs_{kpos}")
                Vs_b[kpos] = rb
                k.dma(sp, lambda e: e.dma_start(out=Vs[kpos:kpos + T, :].rearrange("(b p) c -> p b c", p=128),
                                                in_=V_sb[:, 0:nb, :]), V_sb_b, reads=[V_sb_b], writes=[rb])
            else:
                rb = Buf("Vs_meta")
                Vs_b[cfg.MP] = rb
                k.dma(sp, lambda e: e.dma_start(out=Vs[cfg.MP:cfg.MP + 16, :], in_=V_sb[40:56, 0, :]),
                      V_sb_b, reads=[V_sb_b], writes=[rb])
        if mode in ("halo", "own"):
            sgc = load_slab("win", 2048, 512)
            shc = load_slab("win", 2560, 512)
            for cc in range(4):
                p1 = rr("ps", 8)
                p2 = rr("ps", 8)
                mm_fm(sgc, cc * 128, p1, T)
                mm_fm(shc, cc * 128, p2, T)
                j = rr("gcs", 2)
                k.op(actE, lambda e, p=p1, j=j: e.activation(out=gcs[j][:, 0:T], in_=PS[p][:, 0:T], func=AF.Copy),
                     reads=[PS_b[p1]], writes=[gcs_b[j]])
                if mode == "halo":
                    k.op(dve, lambda e, p=p2, j=j, cc=cc: e.tensor_tensor(out=halo_u[:, cc, 0:T], in0=gcs[j][:, 0:T],
                                                                          in1=PS[p][:, 0:T], op=ALU.mult),
                         reads=[gcs_b[j], PS_b[p2]], writes=[halo_b])
                elif smp:
                    k.op(dve, lambda e, p=p2, j=j, cc=cc: e.tensor_tensor(
                        out=uT[:, cc, 0:136].rearrange("p (s t) -> p s t", t=34)[:, :, 2:34],
                        in0=gcs[j][:, 0:128].rearrange("p (s t) -> p s t", t=32),
                        in1=PS[p][:, 0:128].rearrange("p (s t) -> p s t", t=32), op=ALU.mult),
                         reads=[gcs_b[j], PS_b[p2]], writes=[uT_b[cc]])
                    k.op(dve, lambda e, cc=cc: e.tensor_copy(out=halo_u[:, cc, 0:2 * NR],
                                                             in_=uT[:, cc, 2 + 8:2 + 8 + 2 * NR]),
                         reads=[uT_b[cc]], writes=[halo_b])
                else:
                    k.op(dve, lambda e, p=p2, j=j, cc=cc: e.tensor_tensor(out=uT[:, cc, 2:2 + T], in0=gcs[j][:, 0:T],
                                                                          in1=PS[p][:, 0:T], op=ALU.mult),
                         reads=[gcs_b[j], PS_b[p2]], writes=[uT_b[cc]])
        if not own:
            return
        sq = load_slab("win", 0, 512)
        for h in range(NH):
            p = rr("ps", 8)
            mm_fm(sq, h * 128, p, T)
            k.op(actE, lambda e, h=h, p=p: e.activation(out=qT[:, h, 0:T], in_=PS[p][:, 0:T], func=AF.Copy, scale=0.125),
                 reads=[PS_b[p]], writes=[qT_b[h]])
        sgb = load_slab("win", 1536, 512)
        for cc in range(4):
            if not smp:
                k.op(dve, lambda e, cc=cc: e.tensor_copy(out=uT[:, cc, 0:2], in_=halo_u[:, cc, halo_cols:halo_cols + 2]),
                     reads=[halo_b], writes=[uT_b[cc]])
            else:
                for s_ in range(SB):
                    k.dma(sp, lambda e, cc=cc, s_=s_: e.dma_start(
                        out=uT[:, cc, s_ * 34:s_ * 34 + 2],
                        in_=sconv_in[2 * s_:2 * s_ + 2, cc * 128:(cc + 1) * 128].rearrange("j p -> p j")),
                          uT_b[cc], writes=[uT_b[cc]])
            a = rr("cvt", 2)
            if not smp:
                def uv(o):
                    return uT[:, cc, o:o + T]
                cva = cvt[a][:, 0:T]
            else:
                def uv(o, cc=cc):
                    return uT[:, cc, 0:4 * 34].rearrange("p (s t) -> p s t", t=34)[:, :, o:o + 32]
                cva = cvt[a][:, 0:128].rearrange("p (s t) -> p s t", t=32)
            k.op(dve, lambda e, cc=cc: e.tensor_scalar(out=cva, in0=uv(0), scalar1=convw[:, cc, 0:1], scalar2=None,
                                                       op0=ALU.mult), reads=[uT_b[cc], const_b], writes=[cvt_b[a]])
            k.op(dve, lambda e, cc=cc: e.scalar_tensor_tensor(out=cva, in0=uv(1), scalar=convw[:, cc, 1:2], in1=cva,
                                                              op0=ALU.mult, op1=ALU.add),
                 reads=[uT_b[cc], const_b, cvt_b[a]], writes=[cvt_b[a]])
            k.op(dve, lambda e, cc=cc: e.scalar_tensor_tensor(out=cva, in0=uv(2), scalar=convw[:, cc, 2:3], in1=cva,
                                                              op0=ALU.mult, op1=ALU.add),
                 reads=[uT_b[cc], const_b, cvt_b[a]], writes=[cvt_b[a]])
            p = rr("ps", 8)
            mm_fm(sgb, cc * 128, p, T)
            k.op(dve, lambda e, cc=cc, p=p, a=a: e.tensor_tensor(out=catT[:, 4 + cc, 0:T], in0=cvt[a][:, 0:T],
                                                                 in1=PS[p][:, 0:T], op=ALU.mult),
                 reads=[cvt_b[a], PS_b[p]], writes=[catT_b[4 + cc]])
        if "conv" in outs and not smp:
            for j2 in range(2):
                k.dma(sp, lambda e, j2=j2: e.dma_start(out=outs["conv"][j2:j2 + 1, :].rearrange("o (c p) -> p (o c)", p=128),
                                                       in_=uT[:, :, 2 + T - 2 + j2]),
                      uT_b[0], reads=uT_b, is_out=True)
        if smp:
            for cc in range(4):
                for s_ in range(SB):
                    k.dma(sp, lambda e, cc=cc, s_=s_: e.dma_start(
                        out=conv_smp[2 * s_:2 * s_ + 2, cc * 128:(cc + 1) * 128].rearrange("j p -> p j"),
                        in_=uT[:, cc, s_ * 34 + 8:s_ * 34 + 10]),
                          uT_b[cc], reads=[uT_b[cc]], is_out=True)
        if smp:
            sample_attention()
        else:
            prompt_attention(nb, outs["kchunks"])
        pys = {}
        for half in range(2):
            so = load_slab("wout", half * 512, 512)
            for blk in range(nb):
                p = rr("ps", 8)
                pys[(blk, half)] = p

                def f(e, blk=blk, p=p, so=so):
                    last = None
                    for kk in range(8):
                        last = e.matmul(out=PS[p][:], lhsT=catT[:, kk, blk * 128:(blk + 1) * 128], rhs=slab[so][:, kk, :],
                                        start=(kk == 0), stop=(kk == 7))
                    return last
                k.op(pe, f, reads=[slab_b[so]] + catT_b, writes=[PS_b[p]])
        for blk in range(nb):
            post_norm_residual(xi, blk, pys[(blk, 0)], pys[(blk, 1)], "mpost", 1.0)
        ffn(xi, nb, "f2g", "f2u", "f2d", "f2pre", "f2post")
        for (r0, r1, dst) in outs["y"]:
            for blk in range(nb):
                lo, hi = max(r0, blk * 128), min(r1, (blk + 1) * 128)
                if lo >= hi:
                    continue
                k.dma(sp, lambda e, blk=blk, lo=lo, hi=hi: e.dma_start(
                    out=dst[lo - r0:hi - r0, :], in_=xt[xi][lo - blk * 128:hi - blk * 128, blk, :]),
                      xt_b[xi][blk], reads=[xt_b[xi][blk]], is_out=True)

    def prompt_attention(nb, kchunks):
        T = nb * 128
        for h in range(NH):
            accs = {}
            ring["banks"] = [0, 1, 2, 3, 4]
            state["ps"] = 0
            banks = [5, 6, 7]
            for pb in banks:
                k.op(dve, lambda e, pb=pb: e.memset(PS[pb][:], 0.0), writes=[PS_b[pb]])
            idx = 0
            for qb in range(nb):
                for c in range(2):
                    accs[(qb, c)] = (banks[idx // 3], (idx % 3) * 132)
                    idx += 1
            loaded = {}

            def load_chunk(ci):
                if ci >= len(kchunks) or ci in loaded:
                    return
                kp, nkb, kind = kchunks[ci]
                kc = rr("kc", NKC)
                loaded[ci] = kc
                Tk = nkb * 128
                if kind == "meta":
                    k.dma(sp, lambda e: e.dma_start(out=KTc[kc][:, 0:16], in_=KTs[h * 128:(h + 1) * 128, kp:kp + 16]),
                          KTc_b[kc], reads=[KTs_b[kp]], writes=[KTc_b[kc]])
                    k.dma(sp, lambda e: e.dma_start(out=Vc[kc][0:16, 0, 0:128], in_=Vs[kp:kp + 16, h * 128:(h + 1) * 128]),
                          Vc_b[kc], reads=[Vs_b[kp]], writes=[Vc_b[kc]])
                else:
                    k.dma(sp, lambda e: e.dma_start(out=KTc[kc][:, 0:Tk], in_=KTs[h * 128:(h + 1) * 128, kp:kp + Tk]),
                          KTc_b[kc], reads=[KTs_b[kp]], writes=[KTc_b[kc]])
                    k.dma(sp, lambda e: e.dma_start(
                        out=Vc[kc][:, 0:nkb, 0:128], in_=Vs[kp:kp + Tk, h * 128:(h + 1) * 128].rearrange("(b p) c -> p b c", p=128)),
                          Vc_b[kc], reads=[Vs_b[kp]], writes=[Vc_b[kc]])
                if isinstance(kind, tuple):
                    s_ = kind[1]
                    k.op(dve, lambda e: e.tensor_scalar(out=Vc[kc][:, 0:nkb, 0:128], in0=Vc[kc][:, 0:nkb, 0:128],
                                                        scalar1=sel[:, s_:s_ + 1], scalar2=None, op0=ALU.mult),
                         reads=[Vc_b[kc], const_b], writes=[Vc_b[kc]])
                    k.op(dve, lambda e: e.tensor_scalar(out=Vc[kc][:, 0:nkb, 128:129], in0=ones4[:, 0:nkb, :],
                                                        scalar1=sel[:, s_:s_ + 1], scalar2=None, op0=ALU.mult),
                         reads=[Vc_b[kc], const_b], writes=[Vc_b[kc]])
                else:
                    k.op(dve, lambda e: e.tensor_copy(out=Vc[kc][:, 0:nkb, 128:129], in_=ones4[:, 0:nkb, :]),
                         reads=[const_b], writes=[Vc_b[kc]])

            items = [(ci, kb) for ci, (kp, nkb, kind) in enumerate(kchunks) for kb in range(nkb)]

            def emit_qk(it):
                ci, kb = it
                kp, nkb, kind = kchunks[ci]
                if kb == 0:
                    load_chunk(ci)
                    load_chunk(ci + 1)
                kc = loaded[ci]
                KN = 16 if kind == "meta" else 128
                q0 = kb * 128 if kind == "diag" else 0
                N = T - q0
                pts = []
                for c in range(2):
                    p = rr("ps", 8)
                    k.op(pe, lambda e, c=c, p=p: e.matmul(
                        out=PS[p][0:KN, 0:N], lhsT=KTc[kc][c * 64:(c + 1) * 64, kb * 128:kb * 128 + KN],
                        rhs=qT[c * 64:(c + 1) * 64, h, q0:T], start=True, stop=True),
                         reads=[KTc_b[kc], qT_b[h]], writes=[PS_b[p]])
                    t = rr("pt", 4)
                    k.op(actE, lambda e, p=p, t=t: e.activation(out=PT[t][0:KN, 0:N], in_=PS[p][0:KN, 0:N], func=AF.Exp),
                         reads=[PS_b[p]], writes=[PT_b[t]])
                    if kind == "diag":
                        k.op(pool, lambda e, t=t: e.tensor_tensor(out=PT[t][:, 0:128], in0=PT[t][:, 0:128],
                                                                  in1=mask0[:, 0, 0:128], op=ALU.mult),
                             reads=[PT_b[t], const_b], writes=[PT_b[t]])
                    pts.append(t)
                return (kc, kind, kb, q0, KN, pts)

            def emit_av(st):
                kc, kind, kb, q0, KN, pts = st
                for qb in range(q0 // 128, nb):
                    if kind == "diag" and kb > qb:
                        continue
                    for c in range(2):
                        pb, off = accs[(qb, c)]
                        k.op(pe, lambda e, t=pts[c], qb=qb, pb=pb, off=off: e.matmul(
                            out=PS[pb][:, off:off + 129], lhsT=PT[t][0:KN, qb * 128 - q0:(qb + 1) * 128 - q0],
                            rhs=Vc[kc][0:KN, kb, 0:129], start=False, stop=False, skip_group_check=True),
                             reads=[PT_b[pts[c]], Vc_b[kc]], writes=[PS_b[pb]])

            pend = emit_qk(items[0])
            for ii in range(len(items)):
                nxt = emit_qk(items[ii + 1]) if ii + 1 < len(items) else None
                emit_av(pend)
                pend = nxt
            for qb in range(nb):
                finalize_head(h, qb, accs[(qb, 0)], accs[(qb, 1)], 128)
        ring["banks"] = list(range(8))
        state["ps"] = 0

    def finalize_head(h, qb, a0, a1, nrows):
        (pb0, o0), (pb1, o1) = a0, a1
        st, st_b = new_stat()
        R = slice(0, nrows)
        k.op(dve, lambda e: e.tensor_copy(out=st[R, 0:1], in_=PS[pb0][R, o0 + 128:o0 + 129]), reads=[PS_b[pb0]], writes=[st_b])
        k.op(dve, lambda e: e.tensor_copy(out=st[R, 1:2], in_=PS[pb1][R, o1 + 128:o1 + 129]), reads=[PS_b[pb1]], writes=[st_b])
        k.op(dve, lambda e: e.reciprocal(out=st[R, 0:2], in_=st[R, 0:2]), reads=[st_b], writes=[st_b])
        k.op(dve, lambda e: e.tensor_tensor(out=st[R, 1:2], in0=st[R, 1:2], in1=lamt[R, 4:5], op=ALU.mult),
             reads=[st_b, const_b], writes=[st_b])
        j = rr("osb", 2)
        k.op(dve, lambda e: e.tensor_scalar(out=osb[j][R, :], in0=PS[pb0][R, o0:o0 + 128], scalar1=st[R, 0:1], scalar2=None,
                                            op0=ALU.mult), reads=[PS_b[pb0], st_b], writes=[osb_b[j]])
        k.op(dve, lambda e: e.scalar_tensor_tensor(out=osb[j][R, :], in0=PS[pb1][R, o1:o1 + 128], scalar=st[R, 1:2], in1=osb[j][R, :],
                                                   op0=ALU.mult, op1=ALU.add),
             reads=[PS_b[pb1], st_b, osb_b[j]], writes=[osb_b[j]])
        k.op(actE, lambda e: e.activation(out=junk[R, 0:128], in_=osb[j][R, :], func=AF.Square, accum_out=st[R, 2:3]),
             reads=[osb_b[j]], writes=[junk_b, st_b])
        rstd_from_ssq(st[R, 2:3], st_b, 128, SUBLN_EPS, 1.0, 1, rows=nrows)
        k.op(dve, lambda e: e.scalar_tensor_tensor(out=onb[j][R, :], in0=osb[j][R, :], scalar=st[R, 2:3], in1=subln_g[R, :],
                                                   op0=ALU.mult, op1=ALU.mult),
             reads=[osb_b[j], st_b, const_b], writes=[onb_b[j]])
        return j

    def finalize_prompt(h, qb, j):
        p = rr("ps", 8)
        pbf = PS[p][:].bitcast(BF16)
        k.op(pe, lambda e: e.transpose(out=pbf[:, 0:128], in_=onb[j][:], identity=ident[:]), reads=[onb_b[j], const_b],
             writes=[PS_b[p]])
        k.op(actE, lambda e: e.activation(out=catT[:, h, qb * 128:(qb + 1) * 128], in_=pbf[:, 0:128], func=AF.Copy),
             reads=[PS_b[p]], writes=[catT_b[h]])

    _fh = finalize_head

    def finalize_head(h, qb, a0, a1, nrows):
        j = _fh(h, qb, a0, a1, nrows)
        finalize_prompt(h, qb, j)

    def sample_attention():
        k.op(dve, lambda e: e.tensor_copy(out=Vn[:, :, 0:128], in_=V_sb[:, 0, :].rearrange("p (h c) -> p h c", h=4)),
             reads=[V_sb_b], writes=[Vn_b])
        for h in range(NH):
            k.op(pool, lambda e, h=h: e.memset(catT[:, h, 0:128], 0.0), writes=[catT_b[h]])
        k.op(pool, lambda e: e.memset(Qblk[:], 0.0), writes=[Qblk_b])
        for s_ in range(SB):
            k.op(dve, lambda e, s_=s_: e.tensor_copy(out=Qblk[0:64, s_, :, 0:8], in_=qT[0:64, :, 32 * s_:32 * s_ + 8]),
                 reads=qT_b, writes=[Qblk_b])
            k.op(dve, lambda e, s_=s_: e.tensor_copy(out=Qblk[64:128, s_, :, 8:16], in_=qT[64:128, :, 32 * s_:32 * s_ + 8]),
                 reads=qT_b, writes=[Qblk_b])
        ring["banks"] = [0, 1, 2, 3, 4]
        state["ps"] = 0
        banks = [5, 6, 7]
        for s_ in range(SB):
            for pb in banks:
                k.op(dve, lambda e, pb=pb: e.memset(PS[pb][:], 0.0), writes=[PS_b[pb]])
            accs = {}
            for h in range(NH):
                for c in range(2):
                    i = h * 2 + c
                    accs[(h, c)] = (banks[i // 3], (i % 3) * 132)

            def stage_a(ktile, ktb, mask):
                p = rr("ps", 8)

                def mm(e):
                    last = None
                    for h in range(NH):
                        last = e.matmul(out=PS[p][:, h * 16:(h + 1) * 16], lhsT=ktile[:, h, :], rhs=Qblk[:, s_, h, :],
                                        start=True, stop=True)
                    return last
                k.op(pe, mm, reads=[ktb, Qblk_b], writes=[PS_b[p]])
                t = rr("pts", 2)
                k.op(actE, lambda e: e.activation(out=PTs[t][:], in_=PS[p][:, 0:64], func=AF.Exp), reads=[PS_b[p]], writes=[PTs_b[t]])
                if mask is not None:
                    k.op(dve, lambda e: e.tensor_tensor(out=PTs[t][:], in0=PTs[t][:], in1=mask, op=ALU.mult),
                         reads=[PTs_b[t], const_b], writes=[PTs_b[t]])
                return t

            def stage_b(t, vtile, vtb):
                for h in range(NH):
                    for c in range(2):
                        pb, off = accs[(h, c)]
                        k.op(pe, lambda e, h=h, c=c, pb=pb, off=off: e.matmul(
                            out=PS[pb][0:8, off:off + 129], lhsT=PTs[t][:, h * 16 + c * 8:h * 16 + c * 8 + 8],
                            rhs=vtile[:, h, 0:129], start=False, stop=False, skip_group_check=True),
                             reads=[PTs_b[t], vtb], writes=[PS_b[pb]])

            ck2 = cache_k.ap().rearrange("(r two) c -> r (two c)", two=2)
            cv2 = cache_v.ap().rearrange("(r two) c -> r (two c)", two=2)
            gathered = {}

            def gather(pp):
                if pp >= NPG // 2 or pp in gathered:
                    return
                col = s_ * (NPG // 2) + pp
                gi = rr("kpg", 2)
                gathered[pp] = gi
                k.dma(pool, lambda e: e.indirect_dma_start(
                    out=kpg[gi][:].rearrange("p j c -> p (j c)"), out_offset=None, in_=ck2,
                    in_offset=bass.IndirectOffsetOnAxis(ap=idx[:, col:col + 1], axis=0)),
                      kpg_b[gi], reads=[const_b], writes=[kpg_b[gi]])
                k.dma(pool, lambda e: e.indirect_dma_start(
                    out=vpg[gi][:].rearrange("p j c -> p (j c)"), out_offset=None, in_=cv2,
                    in_offset=bass.IndirectOffsetOnAxis(ap=idx[:, col:col + 1], axis=0)),
                      vpg_b[gi], reads=[const_b], writes=[vpg_b[gi]])

            def front(item):
                if item is None:
                    t = stage_a(KT_sb[:, :, 0:128], KT_sb_b, smask[:, s_, :])
                    return (t, Vn, Vn_b)
                pp, jj = item
                if jj == 0:
                    gather(pp)
                    gather(pp + 1)
                gi = gathered[pp]
                kt = rr("ktp", 2)
                p = rr("ps", 8)

                def tr(e):
                    last = None
                    for h in range(NH):
                        last = e.transpose(out=PS[p][:, h * 128:(h + 1) * 128], in_=kpg[gi][:, jj, h * 128:(h + 1) * 128],
                                           identity=ident_f[:])
                    return last
                k.op(pe, tr, reads=[kpg_b[gi], const_b], writes=[PS_b[p]])
                k.op(actE, lambda e: e.activation(out=KTp[kt][:].rearrange("p h k -> p (h k)"), in_=PS[p][:], func=AF.Copy),
                     reads=[PS_b[p]], writes=[KTp_b[kt]])
                k.op(dve, lambda e: e.tensor_copy(out=Vp[kt][:, :, 0:128],
                                                  in_=vpg[gi][:, jj, :].rearrange("p (h c) -> p h c", h=4)),
                     reads=[vpg_b[gi]], writes=[Vp_b[kt]])
                t = stage_a(KTp[kt], KTp_b[kt], None)
                return (t, Vp[kt], Vp_b[kt])

            sitems = [(pp, jj) for pp in range(NPG // 2) for jj in range(2)] + [None]
            pend = front(sitems[0])
            for ii in range(len(sitems)):
                nxt = front(sitems[ii + 1]) if ii + 1 < len(sitems) else None
                stage_b(*pend)
                pend = nxt
            for h in range(NH):
                j = _fh(h, 0, accs[(h, 0)], accs[(h, 1)], 8)
                p = rr("ps", 8)
                pbf = PS[p][:].bitcast(BF16)
                k.op(pe, lambda e, j=j, pbf=pbf: e.transpose(out=pbf[:, 0:8], in_=onb[j][0:8, :], identity=ident[0:8, 0:8]),
                     reads=[onb_b[j], const_b], writes=[PS_b[p]])
                k.op(actE, lambda e, h=h, pbf=pbf: e.activation(out=catT[:, h, 32 * s_:32 * s_ + 8], in_=pbf[:, 0:8], func=AF.Copy),
                     reads=[PS_b[p]], writes=[catT_b[h]])
        ring["banks"] = list(range(8))
        state["ps"] = 0

    xi = 0
    if cfg.do_sample:
        outs = {"k": k_smp.ap(), "v": v_smp.ap(), "y": [(0, 128, y_smp.ap())]}
        tile_pass(x_smp.ap(), 1, "own", xi, outs=outs, smp=True)
    else:
        raise NotImplementedError("prompt-only build no longer supported (meta keys come from the sample block)")
    for r in range(NR):
        base = r * 2048
        for s in range(1, 4):
            tile_pass(x_pre[(r * 3 + s - 1) * 512:(r * 3 + s) * 512, :], 4, "pre", xi, kpos=base + 512 * s)
            xi ^= 1
        kch = [(cfg.MP, 1, "meta")]
        for rr_ in range(r):
            for s in range(4):
                kch.append((rr_ * 2048 + 512 * s, 4, "full"))
        for s in range(1, 4):
            kch.append((base + 512 * s, 4, ("sel", s)))
        kch.append((base, 4, "diag"))
        outs = {"k": k_own[r * 512:(r + 1) * 512, :], "v": v_own[r * 512:(r + 1) * 512, :],
                "y": [(0, 512, y_own[r * 512:(r + 1) * 512, :])], "kchunks": kch}
        if r == NR - 1:
            outs["conv"] = conv_last.ap()
        tile_pass(x_own[r * 512:(r + 1) * 512, :], 4, "own", xi, kpos=base, outs=outs, halo_cols=2 * r)
        xi ^= 1
    k.finish()
    print("total ops", k.nops, "sems", k.nsem, "cnt pe/act/dve/pool", k.pe.cnt, k.act.cnt, k.dve.cnt, k.pool.cnt)
    return nc, es


def _bf16(a):
    return np.asarray(a, dtype=np.float32).astype(ml_dtypes.bfloat16)


def make_in_maps(cfg, inp):
    NR = cfg.NR
    B = cfg.BATCH
    xp = np.asarray(inp["x_prompt"], dtype=np.float32)
    meta = np.asarray(inp["meta_tokens"], dtype=np.float32)
    maps = []
    kk = np.arange(128)[:, None, None] + 128 * np.arange(4)[None, :, None]
    qq = np.arange(512)[None, None, :]
    dm = (qq >= kk).astype(np.float32).reshape(128, 2048)
    dmask = _bf16(dm)
    common = {}
    for src, dst in [("ffn1_w_gate", "f1g"), ("ffn1_w_up", "f1u"), ("ffn1_w_down", "f1d"), ("w_in", "win"),
                     ("w_out", "wout"), ("ffn2_w_gate", "f2g"), ("ffn2_w_up", "f2u"), ("ffn2_w_down", "f2d")]:
        common[dst] = np.ascontiguousarray(np.asarray(inp[src], dtype=np.float32)[0])
    for src, dst in [("ffn1_pre_g", "f1pre"), ("ffn1_post_g", "f1post"), ("mix_pre_g", "mpre"), ("mix_post_g", "mpost"),
                     ("ffn2_pre_g", "f2pre"), ("ffn2_post_g", "f2post")]:
        common[dst] = np.asarray(inp[src], dtype=np.float32).reshape(1, D)
    common["subln"] = np.asarray(inp["subln_g"], dtype=np.float32).reshape(1, 128)
    common["convw"] = np.ascontiguousarray(np.asarray(inp["conv_w"], dtype=np.float32)[0])
    common["lamv"] = np.concatenate([np.asarray(inp[n], dtype=np.float32).reshape(-1) for n in
                                     ["lambda_q1", "lambda_k1", "lambda_q2", "lambda_k2"]]).reshape(1, 256)
    common["dmask"] = dmask
    rows = np.arange(128)[:, None, None]
    ss = np.arange(4)[None, :, None]
    qcol = (np.arange(64) % 8)[None, None, :]
    t_k = rows - 32 * ss
    sm = ((t_k >= 0) & (t_k < 8) & (t_k <= qcol)).astype(np.float32).reshape(128, 256)
    common["smask"] = _bf16(sm)
    if cfg.do_sample:
        common["cache_k"] = np.asarray(inp["cache_k"], dtype=np.float32).reshape(cfg.NPOOL * 128, 512)
        common["cache_v"] = np.asarray(inp["cache_v"], dtype=np.float32).reshape(cfg.NPOOL * 128, 512)
    xs_all = np.asarray(inp["x_sample"], dtype=np.float32)
    pt_all = np.asarray(inp["page_table"]).astype(np.int32)
    sc_all = np.asarray(inp["state_conv"], dtype=np.float32)[0]
    for c in range(8):
        b, j = c // 4, c % 4
        b = min(b, B - 1)
        seq = xp[b]
        m = dict(common)
        x_pre = np.zeros((NR * 3 * 512, D), np.float32)
        x_own = np.zeros((NR * 512, D), np.float32)
        x_halo = np.zeros((128, D), np.float32)
        sel = np.zeros((128, 4), np.float32)
        for r in range(NR):
            others = [t for t in range(4) if t != j]
            for s, t in enumerate(others):
                g = 4 * r + t
                x_pre[(r * 3 + s) * 512:(r * 3 + s + 1) * 512] = seq[g * 512:(g + 1) * 512]
            g = 4 * r + j
            x_own[r * 512:(r + 1) * 512] = seq[g * 512:(g + 1) * 512]
            if g > 0:
                x_halo[2 * r:2 * r + 2] = seq[g * 512 - 2:g * 512]
            else:
                x_halo[2 * r:2 * r + 2] = meta[N_META - 2:N_META]
        others = [t for t in range(4) if t != j]
        for s, t in enumerate(others):
            sel[:, s + 1] = 1.0 if t < j else 0.0
        x_smp = np.zeros((128, D), np.float32)
        for s_ in range(cfg.SB):
            x_smp[32 * s_:32 * s_ + 8] = xs_all[cfg.SB * c + s_]
        x_smp[8:8 + 2 * NR] = x_halo[0:2 * NR]
        x_smp[40:40 + N_META] = meta
        m.update({"x_pre": x_pre, "x_own": x_own, "x_halo": x_halo, "x_smp": x_smp, "sel": sel})
        if cfg.do_sample:
            m["pt"] = np.ascontiguousarray(pt_all[cfg.SB * c:cfg.SB * (c + 1)]).reshape(1, -1)
            m["sconv"] = np.ascontiguousarray(sc_all[cfg.SB * c:cfg.SB * (c + 1)]).reshape(cfg.SB * 2, 512)
        maps.append(m)
    return maps


def assemble(cfg, res, inp):
    NR, B = cfg.NR, cfg.BATCH
    L = cfg.L
    yp = np.zeros((B, cfg.SEQ, D), np.float32)
    kp = np.zeros((B, L, 512), np.float32)
    vp = np.zeros((B, L, 512), np.float32)
    cp = np.zeros((1, B, 2, 512), np.float32)
    for c in range(8):
        b, j = c // 4, c % 4
        if b >= B:
            continue
        r_ = res[c]
        for r in range(NR):
            g = 4 * r + j
            yp[b, g * 512:(g + 1) * 512] = r_["y_own"][r * 512:(r + 1) * 512]
            kp[b, N_META + g * 512:N_META + (g + 1) * 512] = r_["k_own"][r * 512:(r + 1) * 512]
            vp[b, N_META + g * 512:N_META + (g + 1) * 512] = r_["v_own"][r * 512:(r + 1) * 512]
        if j == 0:
            kp[b, :N_META] = r_["k_smp"][40:40 + N_META]
            vp[b, :N_META] = r_["v_smp"][40:40 + N_META]
        if j == 3:
            cp[0, b] = r_["conv_last"]
    DB = cfg.DEC_BATCH
    ys = np.zeros((DB, 8, D), np.float32)
    ks = np.zeros((1, DB, 8, 512), np.float32)
    vs = np.zeros((1, DB, 8, 512), np.float32)
    cs = np.zeros((1, DB, 2, 512), np.float32)
    for c in range(8):
        for s_ in range(cfg.SB):
            ys[cfg.SB * c + s_] = res[c]["y_smp"][32 * s_:32 * s_ + 8]
            ks[0, cfg.SB * c + s_] = res[c]["k_smp"][32 * s_:32 * s_ + 8]
            vs[0, cfg.SB * c + s_] = res[c]["v_smp"][32 * s_:32 * s_ + 8]
            cs[0, cfg.SB * c + s_] = res[c]["conv_smp"][2 * s_:2 * s_ + 2]
    k_prompt = kp.reshape(1, B, L, 4, 2, 64)
    v_prompt = vp.reshape(1, B, L, 4, 128)
    return (yp, ys, k_prompt, v_prompt, cp, ks.reshape(1, DB, 8, 4, 2, 64), vs.reshape(1, DB, 8, 4, 128), cs)


def run(cfg, inp, trace=False):
    nc, es = build(cfg)
    with es:
        maps = make_in_maps(cfg, inp)
        res = run_bass_kernel_spmd(nc, maps, core_ids=list(range(8)), trace=trace)
    return res


def kernel(**inputs):
    cfg = Cfg()
    res = run(cfg, inputs)
    outs = assemble(cfg, res.results, inputs)
    return outs
```
